# Optimizing a Trainium2 kernel written in Bass

```python
import math
import jax, jax.numpy as jnp
from jax import lax
import numpy as np

D_MODEL = 1024
BATCH = 4
SEQ = 8192
DEPTH = 2

HEAD_DIM = 64
ROT_DIM = HEAD_DIM // 4
ROPE_THETA = 500000.0
Q_BLOCK = 128
EPS = 1e-6

DIFF_HEADS = 4
DIFF_VDIM = 2 * HEAD_DIM
A_Q = DIFF_HEADS * 2 * HEAD_DIM
A_K = DIFF_HEADS * 2 * HEAD_DIM
A_V = DIFF_HEADS * DIFF_VDIM
FOX_HEADS = 8
B_Q = FOX_HEADS * HEAD_DIM
B_K = FOX_HEADS * HEAD_DIM
B_V = FOX_HEADS * HEAD_DIM
B_F = FOX_HEADS
EVEN_WIDTHS = (A_Q, A_K, A_V, B_Q, B_K, B_V, B_F)
EVEN_IN = A_Q + A_K + A_V + B_Q + B_K + B_V + B_F
EVEN_MIX = A_V + B_V

DSA_HEADS = 16
C_Q = DSA_HEADS * HEAD_DIM
C_K = DSA_HEADS * HEAD_DIM
C_V = DSA_HEADS * HEAD_DIM
IDX_HEADS = 8
IDX_DIM = 64
IDX_Q = IDX_HEADS * IDX_DIM
IDX_K = IDX_DIM
IDX_W = IDX_HEADS
TOPK_MAX = 256
ODD_WIDTHS = (C_Q, C_K, C_V, IDX_Q, IDX_K, IDX_W)
ODD_IN = C_Q + C_K + C_V + IDX_Q + IDX_K + IDX_W
ODD_MIX = C_V

D_FF = 4 * D_MODEL
N_EVEN = (DEPTH + 1) // 2
N_ODD = DEPTH // 2

kernel_name = "hybrid_diff_fox_dsa_block"


def _split(x, widths):
    outs, off = [], 0
    for w in widths:
        outs.append(x[..., off:off + w])
        off += w
    return outs


def rmsnorm(x, g):
    xf = x.astype(jnp.float32)
    y = xf * lax.rsqrt(jnp.mean(xf * xf, axis=-1, keepdims=True) + EPS) * g.astype(jnp.float32)
    return y.astype(x.dtype)


def layernorm(x, g, b):
    xf = x.astype(jnp.float32)
    mu = jnp.mean(xf, axis=-1, keepdims=True)
    var = jnp.mean(jnp.square(xf - mu), axis=-1, keepdims=True)
    y = (xf - mu) * lax.rsqrt(var + EPS) * g.astype(jnp.float32) + b.astype(jnp.float32)
    return y.astype(x.dtype)


def rope_tables(seq):
    pos = jnp.arange(seq, dtype=jnp.float32)
    inv_freq = ROPE_THETA ** (-jnp.arange(0, ROT_DIM, 2, dtype=jnp.float32) / ROT_DIM)
    ang = pos[:, None] * inv_freq[None, :]
    return jnp.cos(ang)[:, None, :], jnp.sin(ang)[:, None, :]


def partial_rope(x, cos, sin):
    xf = x.astype(jnp.float32)
    half = ROT_DIM // 2
    x1, x2, xp = xf[..., :half], xf[..., half:ROT_DIM], xf[..., ROT_DIM:]
    y = jnp.concatenate([x1 * cos - x2 * sin, x2 * cos + x1 * sin, xp], axis=-1)
    return y.astype(x.dtype)


def _unblock(o, b, s):
    return jnp.moveaxis(o, 0, 1).reshape((b, s) + o.shape[3:])


def even_mixer(h, w_in, b_f, lq1, lk1, lq2, lk2, sub_g, w_out, lambda_init, cos, sin):
    b, s, _ = h.shape
    proj = h @ w_in
    a_q, a_k, a_v, f_q, f_k, f_v, f_logit = _split(proj, EVEN_WIDTHS)
    a_q = partial_rope(a_q.reshape(b, s, DIFF_HEADS * 2, HEAD_DIM), cos, sin).reshape(b, s, DIFF_HEADS, 2, HEAD_DIM)
    a_k = partial_rope(a_k.reshape(b, s, DIFF_HEADS * 2, HEAD_DIM), cos, sin).reshape(b, s, DIFF_HEADS, 2, HEAD_DIM)
    a_v = a_v.reshape(b, s, DIFF_HEADS, DIFF_VDIM)
    lam = (jnp.exp(jnp.sum(lq1.astype(jnp.float32) * lk1.astype(jnp.float32)))
           - jnp.exp(jnp.sum(lq2.astype(jnp.float32) * lk2.astype(jnp.float32)))
           + lambda_init)
    f_q = f_q.reshape(b, s, FOX_HEADS, HEAD_DIM)
    f_k = f_k.reshape(b, s, FOX_HEADS, HEAD_DIM)
    f_v = f_v.reshape(b, s, FOX_HEADS, HEAD_DIM)
    log_f = jax.nn.log_sigmoid(f_logit.astype(jnp.float32) + b_f.astype(jnp.float32))
    c = jnp.transpose(jnp.cumsum(log_f, axis=1), (0, 2, 1))

    scale = HEAD_DIM ** -0.5
    kpos = jnp.arange(s)

    def block(i):
        start = i * Q_BLOCK
        qpos = start + jnp.arange(Q_BLOCK)
        causal = kpos[None, :] <= qpos[:, None]
        qa = lax.dynamic_slice_in_dim(a_q, start, Q_BLOCK, axis=1)
        sa = jnp.einsum('bqhcd,bkhcd->bhcqk', qa, a_k).astype(jnp.float32) * scale
        pa = jax.nn.softmax(jnp.where(causal, sa, -jnp.inf), axis=-1)
        attn = pa[:, :, 0] - lam * pa[:, :, 1]
        oa = jnp.einsum('bhqk,bkhe->bqhe', attn, a_v)
        qb = lax.dynamic_slice_in_dim(f_q, start, Q_BLOCK, axis=1)
        cq = lax.dynamic_slice_in_dim(c, start, Q_BLOCK, axis=2)
        sb = jnp.einsum('bqhd,bkhd->bhqk', qb, f_k).astype(jnp.float32) * scale
        sb = sb + cq[..., None] - c[:, :, None, :]
        pb = jax.nn.softmax(jnp.where(causal, sb, -jnp.inf), axis=-1)
        ob = jnp.einsum('bhqk,bkhd->bqhd', pb, f_v)
        return oa.astype(h.dtype), ob.astype(h.dtype)

    oa, ob = lax.map(block, jnp.arange(s // Q_BLOCK))
    oa = _unblock(oa, b, s)
    ob = _unblock(ob, b, s)
    oa = (rmsnorm(oa, sub_g) * (1.0 - lambda_init)).astype(h.dtype)
    mixed = jnp.concatenate([oa.reshape(b, s, A_V), ob.reshape(b, s, B_V)], axis=-1)
    return mixed @ w_out


def odd_mixer(h, w_in, ln_g, ln_b, w_out, cos, sin):
    b, s, _ = h.shape
    proj = h @ w_in
    q, k, v, iq, ik, iw = _split(proj, ODD_WIDTHS)
    q = partial_rope(q.reshape(b, s, DSA_HEADS, HEAD_DIM), cos, sin)
    k = partial_rope(k.reshape(b, s, DSA_HEADS, HEAD_DIM), cos, sin)
    v = v.reshape(b, s, DSA_HEADS, HEAD_DIM)
    iq = partial_rope(iq.reshape(b, s, IDX_HEADS, IDX_DIM), cos, sin)
    ik = partial_rope(layernorm(ik, ln_g, ln_b)[:, :, None, :], cos, sin)[:, :, 0, :]
    iw = iw.astype(jnp.float32) * IDX_HEADS ** -0.5
    k_sel = min(TOPK_MAX, s // 4)
    scale = HEAD_DIM ** -0.5
    kpos = jnp.arange(s)
    gather = jax.vmap(lambda arr, ids: arr[ids])

    def block(i):
        start = i * Q_BLOCK
        qpos = start + jnp.arange(Q_BLOCK)
        causal = kpos[None, :] <= qpos[:, None]
        iqb = lax.dynamic_slice_in_dim(iq, start, Q_BLOCK, axis=1)
        iwb = lax.dynamic_slice_in_dim(iw, start, Q_BLOCK, axis=1)
        logits = jnp.einsum('bqhd,bkd->bqhk', iqb, ik).astype(jnp.float32) * IDX_DIM ** -0.5
        score = jnp.einsum('bqhk,bqh->bqk', jax.nn.relu(logits), iwb)
        score = jnp.where(causal[None], score, -jnp.inf)
        _, idx = lax.top_k(score, k_sel)
        valid = idx <= qpos[None, :, None]
        kg = gather(k, idx)
        vg = gather(v, idx)
        qb = lax.dynamic_slice_in_dim(q, start, Q_BLOCK, axis=1)
        sc = jnp.einsum('bqhd,bqjhd->bhqj', qb, kg).astype(jnp.float32) * scale
        p = jax.nn.softmax(jnp.where(valid[:, None], sc, -jnp.inf), axis=-1)
        o = jnp.einsum('bhqj,bqjhd->bqhd', p, vg)
        return o.astype(h.dtype)

    o = _unblock(lax.map(block, jnp.arange(s // Q_BLOCK)), b, s)
    return o.reshape(b, s, ODD_MIX) @ w_out


def sqrelu_mlp(h, w1, w2):
    return jnp.square(jax.nn.relu(h @ w1)) @ w2


def setup_inputs(seed: int = 0) -> dict:
    key = jax.random.key(seed)
    ks = jax.random.split(key, 19)
    f32 = jnp.float32
    nrm = lambda k, shp, sc: jax.random.normal(k, shp, f32) * sc
    return {
        "x": jax.random.normal(ks[0], (BATCH, SEQ, D_MODEL), f32),
        "norm_mix": 1.0 + nrm(ks[1], (DEPTH, D_MODEL), 0.02),
        "w_in_even": nrm(ks[2], (N_EVEN, D_MODEL, EVEN_IN), D_MODEL ** -0.5),
        "b_forget": jax.random.uniform(ks[3], (N_EVEN, FOX_HEADS), f32, 1.0, 4.0),
        "lambda_q1": nrm(ks[4], (N_EVEN, HEAD_DIM), 0.1),
        "lambda_k1": nrm(ks[5], (N_EVEN, HEAD_DIM), 0.1),
        "lambda_q2": nrm(ks[6], (N_EVEN, HEAD_DIM), 0.1),
        "lambda_k2": nrm(ks[7], (N_EVEN, HEAD_DIM), 0.1),
        "diff_subln_g": 1.0 + nrm(ks[8], (N_EVEN, DIFF_VDIM), 0.02),
        "w_out_even": nrm(ks[9], (N_EVEN, EVEN_MIX, D_MODEL), EVEN_MIX ** -0.5),
        "w_in_odd": nrm(ks[10], (N_ODD, D_MODEL, ODD_IN), D_MODEL ** -0.5),
        "idx_ln_g": 1.0 + nrm(ks[11], (N_ODD, IDX_DIM), 0.02),
        "idx_ln_b": nrm(ks[12], (N_ODD, IDX_DIM), 0.02),
        "w_out_odd": nrm(ks[13], (N_ODD, ODD_MIX, D_MODEL), ODD_MIX ** -0.5),
        "norm_mlp": 1.0 + nrm(ks[14], (DEPTH, D_MODEL), 0.02),
        "w_mlp_in": nrm(ks[15], (DEPTH, D_MODEL, D_FF), D_MODEL ** -0.5),
        "w_mlp_out": nrm(ks[16], (DEPTH, D_FF, D_MODEL), D_FF ** -0.5),
        "norm_final": 1.0 + nrm(ks[17], (D_MODEL,), 0.02),
    }


def reference(x, norm_mix, w_in_even, b_forget, lambda_q1, lambda_k1, lambda_q2, lambda_k2,
              diff_subln_g, w_out_even, w_in_odd, idx_ln_g, idx_ln_b, w_out_odd,
              norm_mlp, w_mlp_in, w_mlp_out, norm_final):
    cos, sin = rope_tables(x.shape[1])
    h = x
    for layer in range(DEPTH):
        j = layer // 2
        hn = rmsnorm(h, norm_mix[layer])
        if layer % 2 == 0:
            lambda_init = 0.8 - 0.6 * math.exp(-0.3 * layer)
            mix = even_mixer(hn, w_in_even[j], b_forget[j], lambda_q1[j], lambda_k1[j],
                             lambda_q2[j], lambda_k2[j], diff_subln_g[j], w_out_even[j],
                             lambda_init, cos, sin)
        else:
            mix = odd_mixer(hn, w_in_odd[j], idx_ln_g[j], idx_ln_b[j], w_out_odd[j], cos, sin)
        h = h + mix.astype(h.dtype)
        h = h + sqrelu_mlp(rmsnorm(h, norm_mlp[layer]), w_mlp_in[layer], w_mlp_out[layer]).astype(h.dtype)
    return rmsnorm(h, norm_final)
```

```python
import contextlib
import numpy as np
import ml_dtypes
import concourse.bass as bass
import concourse.mybir as mybir
from concourse.bass_utils import run_bass_kernel_spmd

F32 = mybir.dt.float32
BF16 = mybir.dt.bfloat16
AF = mybir.ActivationFunctionType
ALU = mybir.AluOpType
AX = mybir.AxisListType

D = 1024
HD = 64
EPS = 1e-6
CH = 512
NCORE = 8


class Buf:
    __slots__ = ("name", "w", "readers")

    def __init__(self, name):
        self.name = name
        self.w = None
        self.readers = []


class Op:
    __slots__ = ("eng", "fn", "deps", "is_dma", "key", "val", "milestone", "idx", "dmaval")

    def __init__(self, eng, fn, is_dma=False, key=None):
        self.eng = eng
        self.fn = fn
        self.deps = []
        self.is_dma = is_dma
        self.key = key
        self.val = 0
        self.milestone = False
        self.dmaval = 0


class Sched:
    ENGS = ("pe", "act", "dve", "pool", "sp")

    def __init__(self, nc, multi=False):
        self.nc = nc
        self.multi = multi
        self.ops = []
        self.dma_cnt = {}

    def buf(self, name):
        return Buf(name)

    def bufs(self, name, n):
        return [Buf(f"{name}{i}") for i in range(n)]

    def _record(self, op, reads, writes):
        deps = []
        for b in reads:
            if b.w is not None:
                deps.append(b.w)
        for b in writes:
            if b.w is not None:
                deps.append(b.w)
            deps.extend(b.readers)
        for b in reads:
            b.readers.append(op)
            if len(b.readers) > 64:
                seen = {}
                for r in b.readers:
                    seen[(r.eng, r.key)] = r
                b.readers = list(seen.values())
        for b in writes:
            b.w = op
            b.readers = []
        uniq = []
        seen = set()
        for d in deps:
            if id(d) in seen or d is op:
                continue
            seen.add(id(d))
            if d.eng == "pe" and op.eng == "pe" and not d.is_dma and not op.is_dma:
                continue
            if d.is_dma and op.is_dma and d.key == op.key:
                continue
            uniq.append(d)
        for d in uniq:
            if d.is_dma:
                op.deps.append((d, self.dma_cnt[d.key]))
            else:
                d.milestone = True
                op.deps.append((d, None))
        self.ops.append(op)
        return op

    def op(self, eng, fn, reads=(), writes=()):
        return self._record(Op(eng, fn), reads, writes)

    def dma(self, eng, out, in_, reads=(), writes=(), key=None, **kw):
        assert key is not None
        o = Op(eng, lambda e: e.dma_start(out=out, in_=in_, **kw), is_dma=True, key=key)
        self.dma_cnt.setdefault(key, 0)
        self._record(o, reads, writes)
        self.dma_cnt[key] += 1
        o.dmaval = self.dma_cnt[key]
        return o

    def emit(self, final_dma_keys=()):
        nc = self.nc
        Sched._ph = getattr(Sched, "_ph", 0) + 1
        ph = Sched._ph
        handles = []
        with contextlib.ExitStack() as es:
            if self.multi:
                esem = {e: nc.alloc_semaphore(name=f"c{ph}_{e}") for e in self.ENGS}
                dsem = {k: nc.alloc_semaphore(name=f"d{ph}_{k}") for k in self.dma_cnt}
                handles = list(esem.values()) + list(dsem.values())
            else:
                esem = {e: es.enter_context(nc.semaphore(f"c_{e}")) for e in self.ENGS}
                dsem = {k: es.enter_context(nc.semaphore(f"d_{k}")) for k in self.dma_cnt}
            cnt = {e: 0 for e in self.ENGS}
            for o in self.ops:
                if not o.is_dma and o.milestone:
                    cnt[o.eng] += 1
                    o.val = cnt[o.eng]
            streams = {e: [] for e in self.ENGS}
            for o in self.ops:
                streams[o.eng].append(o)
            block = es.enter_context(nc.Block())

            def run(engname, eng):
                known = {}
                for o in streams[engname]:
                    need = {}
                    for d, v in o.deps:
                        if d.is_dma:
                            t = ("d", d.key)
                            val = 16 * v
                        else:
                            t = ("e", d.eng)
                            val = d.val
                        if need.get(t, 0) < val:
                            need[t] = val
                    for t, val in need.items():
                        if known.get(t, 0) >= val:
                            continue
                        known[t] = val
                        sem = dsem[t[1]] if t[0] == "d" else esem[t[1]]
                        eng.wait_ge(sem, val)
                    ins = o.fn(eng)
                    if o.is_dma:
                        ins.then_inc(dsem[o.key], 16)
                    elif o.milestone:
                        ins.then_inc(esem[o.eng], 1)
                if engname == "sp":
                    for k in final_dma_keys:
                        eng.wait_ge(dsem[k], 16 * self.dma_cnt[k])

            @block.tensor
            def _(e):
                run("pe", e)

            @block.scalar
            def _(e):
                run("act", e)

            @block.vector
            def _(e):
                run("dve", e)

            @block.gpsimd
            def _(e):
                run("pool", e)

            @block.sync
            def _(e):
                run("sp", e)
        if self.multi:
            nc.clear_and_free_semaphores(handles)
            nc.all_engine_barrier()


def rope_tables_fm(pos, qscale):
    rot = HD // 4
    half = rot // 2
    inv = (500000.0 ** (-np.arange(0, rot, 2, dtype=np.float32) / np.float32(rot))).astype(np.float32)
    ang = pos.astype(np.float32)[None, :] * inv[:, None]
    cos = np.cos(ang).astype(np.float32)
    sin = np.sin(ang).astype(np.float32)
    T = pos.shape[0]
    C = np.ones((128, T), np.float32)
    S = np.zeros((128, T), np.float32)
    for base in (0, 64):
        C[base:base + half] = cos
        C[base + half:base + rot] = cos
        S[base:base + half] = sin
        S[base + half:base + rot] = sin
    return (C * np.float32(qscale)).astype(np.float32), (S * np.float32(qscale)).astype(np.float32)


def own_positions(S, r):
    nch = S // CH
    pos = []
    for j in range(nch // 2):
        g = 2 * j + r
        pos.append(np.arange(g * CH, (g + 1) * CH))
    return np.concatenate(pos)


class Ctx:
    _uid = [0]

    def __init__(self, nc=None, io=None):
        self.fused = nc is not None
        self.nc = nc if nc is not None else bass.Bass("TRN2", target_bir_lowering=False)
        self.S = Sched(self.nc, multi=self.fused)
        self.es = contextlib.ExitStack()
        self.n = 0
        self.io = io or {}
        Ctx._uid[0] += 1
        self.uid = Ctx._uid[0]

    def sb(self, shape, dt, name=None):
        self.n += 1
        return self.es.enter_context(self.nc.sbuf_tensor(f"s{self.uid}_{name}_{self.n}", list(shape), dt))

    def ps(self, shape, dt, name=None):
        self.n += 1
        return self.es.enter_context(self.nc.psum_tensor(f"p{self.uid}_{name}_{self.n}", list(shape), dt))

    def din(self, name, shape, dt=F32):
        if name in self.io:
            ap = self.io[name]
            assert list(ap.shape) == list(shape), (name, ap.shape, shape)
            return ap
        assert not self.fused, name
        return self.nc.dram_tensor(name, list(shape), dt, kind="ExternalInput").ap()

    def dout(self, name, shape, dt=F32):
        if name in self.io:
            ap = self.io[name]
            assert list(ap.shape) == list(shape), (name, ap.shape, shape)
            return ap
        assert not self.fused, name
        return self.nc.dram_tensor(name, list(shape), dt, kind="ExternalOutput").ap()

    def scratch(self, name, shape, dt):
        if name in self.io:
            return self.io[name]
        return self.nc.dram_tensor(f"{name}_{self.uid}", list(shape), dt, kind="Internal").ap()


def emit_norm_T(cx, xt, xt_b, g_bc, ident, hn, hn_b, tp, tp_b, hnT_dst, hnT_b, small, small_b, junk, junk_b, cb=()):
    S = cx.S
    ss, sd, rstd = small
    S.op("act", lambda e: e.activation(out=junk[:], in_=xt[:], func=AF.Square, accum_out=ss[:]),
         reads=[xt_b], writes=[junk_b, small_b])
    S.op("act", lambda e: e.activation(out=sd[:], in_=ss[:], func=AF.Sqrt, scale=1.0 / D, bias=EPS),
         reads=[small_b], writes=[small_b])
    S.op("dve", lambda e: e.reciprocal(out=rstd[:], in_=sd[:]), reads=[small_b], writes=[small_b])
    S.op("dve", lambda e: e.scalar_tensor_tensor(out=hn[:], in0=xt[:], scalar=rstd[:], in1=g_bc[:],
                                                 op0=ALU.mult, op1=ALU.mult),
         reads=[xt_b, small_b] + list(cb), writes=[hn_b])
    for kc in range(8):
        S.op("pe", (lambda kc: lambda e: e.transpose(out=tp[:, kc * 128:(kc + 1) * 128],
                                                    in_=hn[:, kc * 128:(kc + 1) * 128], identity=ident[:]))(kc),
             reads=[hn_b] + list(cb), writes=[tp_b])
    S.op("act", lambda e: e.copy(out=hnT_dst, in_=tp[:].rearrange("p (k t) -> p k t", k=8)),
         reads=[tp_b], writes=[hnT_b])


def inproj_spec(layer):
    if layer == 0:
        return dict(
            C=3080,
            rot=[(0, 1024)],
            fm=[(c, True, True, "qT", c) for c in range(0, 512, 128)]
            + [(1536 + c, False, True, "qT", 512 + c) for c in range(0, 512, 128)]
            + [(512 + c, True, False, "kT", c) for c in range(0, 512, 128)]
            + [(2048 + c, False, False, "kT", 512 + c) for c in range(0, 512, 128)],
            tm=[(1024, 0), (2560, 512)],
            nq=1024,
        )
    return dict(
        C=3656,
        rot=[(0, 2048), (3072, 3584)],
        fm=[(c, True, True, "qT", c) for c in range(0, 1024, 128)]
        + [(1024 + c, True, False, "kT", c) for c in range(0, 1024, 128)]
        + [(3072 + c, True, True, "iqT", c) for c in range(0, 512, 128)],
        tm=[(2048, 0), (2560, 512)],
        nq=1024,
    )


def build_inproj(layer, T, nc=None, io=None):
    cx = Ctx(nc, io)
    nc, S = cx.nc, cx.S
    sp = inproj_spec(layer)
    C = sp["C"]
    nchunk = T // CH
    x = cx.din("x", [T, D])
    w = cx.din("w", [D, C])
    g = cx.din("g", [1, D])
    ident_d = cx.din("ident", [128, 128])
    cq_d = cx.din("cq", [128, T])
    sq_d = cx.din("sq", [128, T])
    ck_d = cx.din("ck", [128, T])
    sk_d = cx.din("sk", [128, T])
    outs = {}
    outs["qT"] = cx.dout("qT", [1024, T], BF16)
    outs["kT"] = cx.dout("kT", [1024, T], BF16)
    outs["v"] = cx.dout("v", [T, 1024], BF16) if layer == 0 else cx.dout("v", [T, 16, 65], BF16)
    if layer == 0:
        bf_d = cx.din("bf", [8, 1])
        outs["nlogf"] = cx.dout("nlogf", [8, T], F32)
    else:
        outs["iqT"] = cx.dout("iqT", [512, T], BF16)
        outs["ikT"] = cx.dout("ikT", [64, T], BF16)
        outs["iw"] = cx.dout("iw", [T, 8], F32)
        lng_d = cx.din("lng", [1, 64])
        lnb_d = cx.din("lnb", [1, 64])
        ctk_d = cx.din("ctk", [T, 8])
        stk_d = cx.din("stk", [T, 8])

    rot_cols = []
    for a, b in sp["rot"]:
        rot_cols.append((a, b, sum(bb - aa for aa, bb in sp["rot"] if bb <= a)))
    nrot = sum(b - a for a, b in sp["rot"])

    def rot_off(col):
        for a, b, off in rot_cols:
            if a <= col < b:
                return off + col - a
        raise KeyError

    with cx.es:
        W = cx.sb([128, 8, C], BF16, "W")
        W2 = cx.sb([128, 8, nrot], BF16, "W2")
        gbc = cx.sb([128, D], F32, "gbc")
        ident = cx.sb([128, 128], BF16, "ident")
        xt = [cx.sb([128, D], F32, f"xt{i}") for i in range(2)]
        junk = cx.sb([128, D], BF16, "junk")
        hn = [cx.sb([128, D], BF16, f"hn{i}") for i in range(2)]
        hnT = [cx.sb([128, 8, CH], BF16, f"hnT{i}") for i in range(2)]
        small = [[cx.sb([128, 1], F32, f"sm{i}_{k}") for k in range(3)] for i in range(2)]
        tabs = [[cx.sb([128, CH], F32, f"tab{i}_{k}") for k in range(4)] for i in range(2)]
        t1 = [cx.sb([128, CH], F32, f"t1_{i}") for i in range(2)]
        t2 = [cx.sb([128, CH], F32, f"t2_{i}") for i in range(2)]
        stg = [cx.sb([128, CH], BF16, f"stg{i}") for i in range(4)]
        tp = [cx.ps([128, D], BF16, f"tp{i}") for i in range(2)]
        fm_ps = [cx.ps([128, CH], F32, f"fm{i}") for i in range(2)]
        fm2_ps = [cx.ps([128, CH], F32, f"fm2{i}") for i in range(2)]
        tm_ps = [cx.ps([128, CH], F32, f"tm{i}") for i in range(2)]

        b_W, b_W2, b_g, b_id = S.buf("W"), S.buf("W2"), S.buf("g"), S.buf("id")
        b_xt, b_junk, b_hn, b_hnT = S.bufs("xt", 2), S.buf("junk"), S.bufs("hn", 2), S.bufs("hnT", 2)
        b_small, b_tabs = S.bufs("small", 2), S.bufs("tabs", 2)
        b_t1, b_t2, b_stg = S.bufs("t1", 2), S.bufs("t2", 2), S.bufs("stg", 4)
        b_tp, b_fm, b_fm2, b_tm = S.bufs("tp", 2), S.bufs("fm", 2), S.bufs("fm2", 2), S.bufs("tm", 2)

        wv = w.rearrange("(k p) c -> p k c", p=128)
        for kc in range(8):
            S.dma("pool", W[:, kc, :], wv[:, kc, :], writes=[b_W], key="W")
        S.dma("pool", ident[:], ident_d, writes=[b_id], key="W")
        S.dma("sp", gbc[:], g.partition_broadcast(128), writes=[b_g], key="g")
        S.op("dve", lambda e: e.memset(W2[:], 0.0), writes=[b_W2])
        for a, b, off in rot_cols:
            nh = (b - a) // HD
            src = W[:, :, a:b].rearrange("p k (h d) -> p k h d", d=HD)
            dst = W2[:, :, off:off + (b - a)].rearrange("p k (h d) -> p k h d", d=HD)
            for kc in range(8):
                S.op("dve", (lambda kc, src, dst: lambda e: e.tensor_scalar(
                    out=dst[:, kc, :, 0:8], in0=src[:, kc, :, 8:16], scalar1=-1.0, scalar2=None, op0=ALU.mult))(kc, src, dst),
                    reads=[b_W], writes=[b_W2])
                S.op("dve", (lambda kc, src, dst: lambda e: e.tensor_copy(
                    out=dst[:, kc, :, 8:16], in_=src[:, kc, :, 0:8]))(kc, src, dst),
                    reads=[b_W], writes=[b_W2])
        if layer == 0:
            negb = cx.sb([8, 1], F32, "negb")
            b_negb = S.buf("negb")
            S.dma("sp", negb[:], bf_d, writes=[b_negb], key="g")
            S.op("dve", lambda e: e.tensor_scalar(out=negb[:], in0=negb[:], scalar1=-1.0, scalar2=None, op0=ALU.mult),
                 reads=[b_negb], writes=[b_negb])
            ef = cx.sb([8, CH], F32, "ef")
            b_ef = S.buf("ef")
            lf = [cx.sb([8, CH], F32, f"lf{i}") for i in range(2)]
            b_lf = S.bufs("lf", 2)
        else:
            lng = cx.sb([128, 64], F32, "lng")
            lnb = cx.sb([128, 64], F32, "lnb")
            b_ln = S.buf("ln")
            S.dma("sp", lng[:], lng_d.partition_broadcast(128), writes=[b_ln], key="g")
            S.dma("sp", lnb[:], lnb_d.partition_broadcast(128), writes=[b_ln], key="g")
            ctk = cx.sb([128, T // 128, 8], F32, "ctk")
            stk = cx.sb([128, T // 128, 8], F32, "stk")
            S.dma("sp", ctk[:], ctk_d.rearrange("(n p) e -> p n e", p=128), writes=[b_ln], key="g")
            S.dma("sp", stk[:], stk_d.rearrange("(n p) e -> p n e", p=128), writes=[b_ln], key="g")
            stgv = [cx.sb([128, 8, 65], BF16, f"stgv{i}") for i in range(2)]
            b_stgv = S.bufs("stgv", 2)
            for i in range(2):
                S.op("pool", (lambda i: lambda e: e.memset(stgv[i][:], 1.0))(i), writes=[b_stgv[i]])
            ik = [cx.sb([128, 72], F32, f"ik{i}") for i in range(2)]
            ikw = [[cx.sb([128, 64], F32, f"ikw{i}_{k}") for k in range(3)] for i in range(2)]
            iks = [[cx.sb([128, 1], F32, f"iks{i}_{k}") for k in range(4)] for i in range(2)]
            ikb = [cx.sb([128, 64], BF16, f"ikb{i}") for i in range(2)]
            ikTs = [cx.sb([64, CH], BF16, f"ikTs{i}") for i in range(2)]
            iwo = [cx.sb([128, 8], F32, f"iwo{i}") for i in range(2)]
            b_ik, b_ikb, b_ikTs, b_iwo = S.bufs("ik", 2), S.bufs("ikb", 2), S.bufs("ikTs", 2), S.bufs("iwo", 2)

        ntile = 0
        nfm = 0
        ntm = 0
        nst = 0
        nsv = 0
        for j in range(nchunk):
            cp = j % 2
            for k, tdram in enumerate((cq_d, sq_d, ck_d, sk_d)):
                S.dma("sp", tabs[cp][k][:], tdram[:, j * CH:(j + 1) * CH], writes=[b_tabs[cp]], key=f"tab{cp}")
            for tt in range(4):
                sl = ntile % 2
                ntile += 1
                r0 = j * CH + tt * 128
                S.dma("sp", xt[sl][:], x[r0:r0 + 128, :], writes=[b_xt[sl]], key=f"x{sl}")
                emit_norm_T(cx, xt[sl], b_xt[sl], gbc, ident, hn[sl], b_hn[sl], tp[sl], b_tp[sl],
                            hnT[cp][:, :, tt * 128:(tt + 1) * 128], b_hnT[cp], small[sl], b_small[sl], junk, b_junk, cb=[b_g, b_id])
                for (c0, oc0) in sp["tm"]:
                    pb = ntm % 2
                    ntm += 1
                    for kc in range(8):
                        S.op("pe", (lambda kc, pb, c0, cp, tt: lambda e: e.matmul(
                            tm_ps[pb][:], lhsT=hnT[cp][:, kc, tt * 128:(tt + 1) * 128], rhs=W[:, kc, c0:c0 + CH],
                            start=(kc == 0), stop=(kc == 7)))(kc, pb, c0, cp, tt),
                            reads=[b_hnT[cp], b_W], writes=[b_tm[pb]])
                    if layer == 0:
                        so = nst % 4
                        nst += 1
                        S.op("act", (lambda pb, so: lambda e: e.copy(out=stg[so][:], in_=tm_ps[pb][:]))(pb, so),
                             reads=[b_tm[pb]], writes=[b_stg[so]])
                        S.dma("sp", outs["v"][r0:r0 + 128, oc0:oc0 + CH], stg[so][:], reads=[b_stg[so]], key=f"st{so}")
                    else:
                        so = nsv % 2
                        nsv += 1
                        S.op("act", (lambda pb, so: lambda e: e.copy(out=stgv[so][:, :, 0:64],
                                                                     in_=tm_ps[pb][:].rearrange("p (h d) -> p h d", d=64)))(pb, so),
                             reads=[b_tm[pb]], writes=[b_stgv[so]])
                        S.dma("sp", outs["v"][r0:r0 + 128, oc0 // 64:oc0 // 64 + 8, :], stgv[so][:], reads=[b_stgv[so]], key=f"sv{so}")
                if layer == 1:
                    q = sl
                    pb = ntm % 2
                    ntm += 1
                    for kc in range(8):
                        S.op("pe", (lambda kc, cp, tt, pb: lambda e: e.matmul(
                            tm_ps[pb][:, 0:72], lhsT=hnT[cp][:, kc, tt * 128:(tt + 1) * 128], rhs=W[:, kc, 3584:3656],
                            start=(kc == 0), stop=(kc == 7)))(kc, cp, tt, pb),
                            reads=[b_hnT[cp], b_W], writes=[b_tm[pb]])
                    S.op("dve", (lambda q, pb: lambda e: e.tensor_copy(out=ik[q][:], in_=tm_ps[pb][:, 0:72]))(q, pb),
                         reads=[b_tm[pb]], writes=[b_ik[q]])
                    S.op("dve", (lambda q: lambda e: e.tensor_scalar(out=iwo[q][:], in0=ik[q][:, 64:72],
                                                                   scalar1=float(8 ** -0.5), scalar2=None, op0=ALU.mult))(q),
                         reads=[b_ik[q]], writes=[b_iwo[q]])
                    S.dma("sp", outs["iw"][r0:r0 + 128, :], iwo[q][:], reads=[b_iwo[q]], key=f"iw{q}")
                    mean, var, rs, nmean = iks[q]
                    xc, sqj, y = ikw[q]
                    S.op("dve", (lambda q, mean: lambda e: e.tensor_scalar(
                        out=ikw[q][1][:], in0=ik[q][:, 0:64], scalar1=1.0 / 64, scalar2=None, op0=ALU.mult,
                        op1=ALU.add, accum_out=mean[:]))(q, mean),
                        reads=[b_ik[q]], writes=[b_ik[q]])
                    S.op("dve", (lambda q, mean, xc: lambda e: e.tensor_scalar(
                        out=xc[:], in0=ik[q][:, 0:64], scalar1=mean[:], scalar2=None, op0=ALU.subtract))(q, mean, xc),
                        reads=[b_ik[q]], writes=[b_ik[q]])
                    S.op("act", (lambda xc, sqj, var: lambda e: e.activation(out=sqj[:], in_=xc[:], func=AF.Square,
                                                                         accum_out=var[:]))(xc, sqj, var),
                         reads=[b_ik[q]], writes=[b_ik[q]])
                    S.op("act", (lambda var, rs: lambda e: e.activation(out=rs[:], in_=var[:], func=AF.Sqrt,
                                                                    scale=1.0 / 64, bias=EPS))(var, rs),
                         reads=[b_ik[q]], writes=[b_ik[q]])
                    S.op("dve", (lambda rs: lambda e: e.reciprocal(out=rs[:], in_=rs[:]))(rs),
                         reads=[b_ik[q]], writes=[b_ik[q]])
                    S.op("dve", (lambda xc, rs, y: lambda e: e.scalar_tensor_tensor(
                        out=y[:], in0=xc[:], scalar=rs[:], in1=lng[:], op0=ALU.mult, op1=ALU.mult))(xc, rs, y),
                        reads=[b_ik[q], b_ln], writes=[b_ik[q]])
                    S.op("dve", (lambda y: lambda e: e.tensor_tensor(out=y[:], in0=y[:], in1=lnb[:], op=ALU.add))(y),
                         reads=[b_ik[q], b_ln], writes=[b_ik[q]])
                    ti = r0 // 128
                    S.op("dve", (lambda y, xc, ti: lambda e: e.tensor_tensor(out=xc[:, 0:8], in0=y[:, 0:8], in1=ctk[:, ti, :], op=ALU.mult))(y, xc, ti),
                         reads=[b_ik[q], b_ln], writes=[b_ik[q]])
                    S.op("dve", (lambda y, xc, ti: lambda e: e.tensor_tensor(out=xc[:, 8:16], in0=y[:, 8:16], in1=stk[:, ti, :], op=ALU.mult))(y, xc, ti),
                         reads=[b_ik[q], b_ln], writes=[b_ik[q]])
                    S.op("dve", (lambda y, xc, ti: lambda e: e.tensor_tensor(out=xc[:, 16:24], in0=y[:, 8:16], in1=ctk[:, ti, :], op=ALU.mult))(y, xc, ti),
                         reads=[b_ik[q], b_ln], writes=[b_ik[q]])
                    S.op("dve", (lambda y, xc, ti: lambda e: e.tensor_tensor(out=xc[:, 24:32], in0=y[:, 0:8], in1=stk[:, ti, :], op=ALU.mult))(y, xc, ti),
                         reads=[b_ik[q], b_ln], writes=[b_ik[q]])
                    S.op("dve", (lambda y, xc: lambda e: e.tensor_tensor(out=y[:, 0:8], in0=xc[:, 0:8], in1=xc[:, 8:16], op=ALU.subtract))(y, xc),
                         reads=[b_ik[q]], writes=[b_ik[q]])
                    S.op("dve", (lambda y, xc: lambda e: e.tensor_tensor(out=y[:, 8:16], in0=xc[:, 16:24], in1=xc[:, 24:32], op=ALU.add))(y, xc),
                         reads=[b_ik[q]], writes=[b_ik[q]])
                    S.op("dve", (lambda y, q: lambda e: e.tensor_copy(out=ikb[q][:], in_=y[:]))(y, q),
                         reads=[b_ik[q]], writes=[b_ikb[q]])
                    S.op("pe", (lambda q, sl: lambda e: e.transpose(out=tp[sl][0:64, 0:128], in_=ikb[q][:], identity=ident[:]))(q, sl),
                         reads=[b_ikb[q], b_id], writes=[b_tp[sl]])
                    S.op("act", (lambda cp, tt, sl: lambda e: e.copy(out=ikTs[cp][:, tt * 128:(tt + 1) * 128], in_=tp[sl][0:64, 0:128]))(cp, tt, sl),
                         reads=[b_tp[sl]], writes=[b_ikTs[cp]])
            if layer == 1:
                S.dma("sp", outs["ikT"][:, j * CH:(j + 1) * CH], ikTs[cp][:], reads=[b_ikTs[cp]], key=f"ikT{cp}")
            for (c0, rot, qs, oname, orow) in sp["fm"]:
                if cx.fused and layer == 1 and j >= nchunk // 2 and oname in ("qT", "iqT"):
                    continue
                pb = nfm % 2
                nfm += 1
                for kc in range(8):
                    S.op("pe", (lambda kc, pb, c0, cp: lambda e: e.matmul(
                        fm_ps[pb][:], lhsT=W[:, kc, c0:c0 + 128], rhs=hnT[cp][:, kc, :],
                        start=(kc == 0), stop=(kc == 7)))(kc, pb, c0, cp),
                        reads=[b_hnT[cp], b_W], writes=[b_fm[pb]])
                so = nst % 4
                nst += 1
                if rot:
                    ro = rot_off(c0)
                    for kc in range(8):
                        S.op("pe", (lambda kc, pb, ro, cp: lambda e: e.matmul(
                            fm2_ps[pb][:], lhsT=W2[:, kc, ro:ro + 128], rhs=hnT[cp][:, kc, :],
                            start=(kc == 0), stop=(kc == 7)))(kc, pb, ro, cp),
                            reads=[b_hnT[cp], b_W2], writes=[b_fm2[pb]])
                    ct, st_ = (tabs[cp][0], tabs[cp][1]) if qs else (tabs[cp][2], tabs[cp][3])
                    S.op("dve", (lambda pb, ct: lambda e: e.tensor_tensor(out=t1[pb][:], in0=fm_ps[pb][:], in1=ct[:], op=ALU.mult))(pb, ct),
                         reads=[b_fm[pb], b_tabs[cp]], writes=[b_t1[pb]])
                    S.op("dve", (lambda pb, st_: lambda e: e.tensor_tensor(out=t2[pb][:], in0=fm2_ps[pb][:], in1=st_[:], op=ALU.mult))(pb, st_),
                         reads=[b_fm2[pb], b_tabs[cp]], writes=[b_t2[pb]])
                    S.op("pool", (lambda pb, so: lambda e: e.tensor_tensor(out=stg[so][:], in0=t1[pb][:], in1=t2[pb][:], op=ALU.add))(pb, so),
                         reads=[b_t1[pb], b_t2[pb]], writes=[b_stg[so]])
                else:
                    sc = 0.125 if qs else 1.0
                    S.op("act", (lambda pb, so, sc: lambda e: e.mul(out=stg[so][:], in_=fm_ps[pb][:], mul=sc))(pb, so, sc),
                         reads=[b_fm[pb]], writes=[b_stg[so]])
                S.dma("sp", outs[oname][orow:orow + 128, j * CH:(j + 1) * CH], stg[so][:], reads=[b_stg[so]], key=f"st{so}")
            if layer == 0:
                pb = nfm % 2
                nfm += 1
                for kc in range(8):
                    S.op("pe", (lambda kc, pb, cp: lambda e: e.matmul(
                        fm_ps[pb][0:8, :], lhsT=W[:, kc, 3072:3080], rhs=hnT[cp][:, kc, :],
                        start=(kc == 0), stop=(kc == 7)))(kc, pb, cp),
                        reads=[b_hnT[cp], b_W], writes=[b_fm[pb]])
                S.op("act", (lambda pb: lambda e: e.activation(out=ef[:], in_=fm_ps[pb][0:8, :], func=AF.Exp, scale=-1.0, bias=negb[:]))(pb),
                     reads=[b_fm[pb], b_negb], writes=[b_ef])
                S.op("act", (lambda cp: lambda e: e.activation(out=lf[cp][:], in_=ef[:], func=AF.Ln, bias=1.0))(cp),
                     reads=[b_ef], writes=[b_lf[cp]])
                S.dma("sp", outs["nlogf"][:, j * CH:(j + 1) * CH], lf[cp][:], reads=[b_lf[cp]], key=f"lf{cp}")
        fin = [k for k in S.dma_cnt if k.startswith(("st", "lf", "iw", "ikT", "sv"))]
        S.emit(final_dma_keys=fin)
    return nc


def zone_masks(r):
    m = np.zeros((8, 128, CH), np.float32)
    k = np.arange(128)[:, None]
    q = np.arange(CH)[None, :]
    for z in range(8):
        m[z] = ((128 * z + k) <= (CH * r + q)).astype(np.float32)
    return m


def zone_negmasks(r):
    return ((zone_masks(r) - 1.0) * 30000.0).astype(np.float32)


class AttnCore:
    def __init__(self, cx, identf, b_identf, tq=None, b_tq=None, tq_stride=129, tq_per_bank=2, npt=4, single_acc=False, n_st=2):
        self.cx = cx
        S = cx.S
        self.st = [cx.ps([128, CH], F32, f"st{i}") for i in range(n_st)]
        self.b_st = S.bufs("st", n_st)
        self.acc = [cx.ps([128, 512], F32, f"acc{i}") for i in range(4)]
        self.b_acc = S.bufs("acc", 4)
        self.sts, self.b_sts = list(self.st), list(self.b_st)
        if single_acc:
            self.sts += [self.acc[1], self.acc[3]]
            self.b_sts += [self.b_acc[1], self.b_acc[3]]
        self.pt = [cx.sb([128, CH], BF16, f"pt{i}") for i in range(npt)]
        self.b_pt = S.bufs("pt", npt)
        self.oT = [cx.sb([128, CH], F32, f"oT{i}") for i in range(4)]
        self.b_oT = S.bufs("oT", 4)
        if tq is None:
            tq = [cx.ps([128, 512], F32, f"tq{i}") for i in range(1)]
            b_tq = S.bufs("tq", 1)
        self.tq, self.b_tq = tq, b_tq
        self.tq_stride, self.tq_per_bank = tq_stride, tq_per_bank
        self.identf, self.b_identf = identf, b_identf
        self.npt = npt
        self.nblk = 0
        self.ngrp = -1
        self.queue = []
        self.look = len(self.sts) - 1

    def set_mode(self, single_acc):
        assert not self.queue
        self.sts, self.b_sts = list(self.st), list(self.b_st)
        if single_acc:
            self.sts += [self.acc[1], self.acc[3]]
            self.b_sts += [self.b_acc[1], self.b_acc[3]]
        self.look = len(self.sts) - 1

    def tq_view(self, qs, ncol):
        b = (qs // self.tq_per_bank) % len(self.tq)
        o = (qs % self.tq_per_bank) * self.tq_stride
        return self.tq[b][:, o:o + ncol], self.b_tq[b]

    def _issue_s(self, blk):
        S = self.cx.S
        i = self.nblk
        self.nblk += 1
        sb = i % len(self.sts)
        blk["sb"] = sb
        blk["pi"] = i % self.npt
        if blk["first"]:
            self.ngrp += 1
        blk["grp"] = self.ngrp
        st = self.sts[sb]
        am = blk.get("addmask")
        S.op("pe", lambda e: e.matmul(st[:], lhsT=blk["k"], rhs=blk["q"], start=True, stop=(am is None)),
             reads=blk["kq_bufs"], writes=[self.b_sts[sb]])
        if am is not None:
            S.op("pe", lambda e: e.matmul(st[:], lhsT=am[0], rhs=am[1], start=False, stop=True),
                 reads=list(am[2]), writes=[self.b_sts[sb]])

    def _finish(self, blk):
        S = self.cx.S
        sb, pi = blk["sb"], blk["pi"]
        st, pt = self.sts[sb], self.pt[pi]
        gb = 2 * (blk["grp"] % 2)
        S.op("act", lambda e: e.activation(out=pt[:], in_=st[:], func=AF.Exp),
             reads=[self.b_sts[sb]], writes=[self.b_pt[pi]])
        for (eng, mk, mb) in blk.get("masks", ()):
            S.op(eng, (lambda mk: lambda e: e.tensor_tensor(out=pt[:], in0=pt[:], in1=mk, op=ALU.mult))(mk),
                 reads=[self.b_pt[pi]] + list(mb), writes=[self.b_pt[pi]])
        for n, (rows, vap, coff) in enumerate(blk["pv"]):
            acc = self.acc[gb + n]
            S.op("pe", (lambda acc, rows, vap: lambda e: e.matmul(acc[0:rows, :], lhsT=vap, rhs=pt[:],
                                                                 start=blk["first"], stop=blk["last"]))(acc, rows, vap),
                 reads=[self.b_pt[pi]] + blk["v_bufs"], writes=[self.b_acc[gb + n]])
        if blk["last"]:
            ncol = sum(r for r, _, _ in blk["pv"])
            for n, (rows, vap, coff) in enumerate(blk["pv"]):
                acc, oT = self.acc[gb + n], self.oT[gb + n]
                S.op("act", (lambda acc, oT, rows: lambda e: e.copy(out=oT[0:rows, :], in_=acc[0:rows, :]))(acc, oT, rows),
                     reads=[self.b_acc[gb + n]], writes=[self.b_oT[gb + n]])
            views = {}
            for pair in ((0, 1), (2, 3)):
                for qs in pair:
                    tv, tb = self.tq_view(qs, ncol)
                    for n, (rows, vap, coff) in enumerate(blk["pv"]):
                        oT = self.oT[gb + n]
                        S.op("pe", (lambda tv, oT, rows, coff, qs: lambda e: e.transpose(
                            out=tv[:, coff:coff + rows], in_=oT[0:rows, qs * 128:(qs + 1) * 128], identity=self.identf[0:rows, 0:rows]))(tv, oT, rows, coff, qs),
                            reads=[self.b_oT[gb + n], self.b_identf], writes=[tb])
                    views[qs] = (tv, tb)
                blk["epilogue"](views, pair)

    def block(self, blk):
        self._issue_s(blk)
        self.queue.append(blk)
        while len(self.queue) > self.look:
            self._finish(self.queue.pop(0))

    def flush(self):
        while self.queue:
            self._finish(self.queue.pop(0))


LAMBDA_INIT0 = 0.8 - 0.6 * float(np.exp(-0.3 * 0))


def ktile_local(j, kt, NT):
    if kt < 4 * j:
        return kt
    if kt < 8 * j:
        return NT // 2 + (kt - 4 * j)
    if kt < 8 * j + 4:
        return 4 * j + (kt - 8 * j)
    return NT // 2 + 4 * j + (kt - 8 * j - 4)


def zone_negmasks_local(r):
    m = np.zeros((2, 8, 128, CH), np.float32)
    k = np.arange(128)[:, None]
    q = np.arange(CH)[None, :]
    for z in range(8):
        tri = np.where((128 * (z % 4) + k) <= q, 0.0, -30000.0)
        if z < 4:
            m[0, z] = tri
            m[1, z] = 0.0 if r == 0 else -30000.0
        else:
            m[0, z] = 0.0 if r == 1 else -30000.0
            m[1, z] = tri
    return m


def build_attn0(SL, nc=None, io=None):
    T = SL
    NT = SL // 128
    NP = SL // 1024
    cx = Ctx(nc, io)
    nc, S = cx.nc, cx.S
    qT = cx.din("qT", [1024, T], BF16)
    kT = cx.din("kT", [1024, SL], BF16)
    v = cx.din("v", [SL, 1024], BF16)
    nlogf = cx.din("nlogf", [8, SL])
    masks_d = cx.din("masks", [2, 8, 128, CH])
    ident_d = cx.din("ident", [128, 128])
    rsel_d = cx.din("rsel", [8, 2])
    lam_d = cx.din("lam", [1, 256])
    subg_d = cx.din("subg", [1, 128])
    mixed = cx.dout("mixed", [T, 1024], BF16)
    cK = cx.scratch("cK", [3, 8, SL], BF16)
    cQ = cx.scratch("cQ", [3, 8, T], BF16)
    PC = 1024

    with cx.es:
        KA = [cx.sb([128, SL], BF16, f"KA{i}") for i in range(2)]
        QA = [cx.sb([128, T], BF16, f"QA{i}") for i in range(2)]
        VA = [cx.sb([128, NT, 129], BF16, f"VA{i}") for i in range(2)]
        VB = [cx.sb([128, NT, 65], BF16, f"VB{i}") for i in range(2)]
        b_KA, b_QA, b_VA, b_VB = S.bufs("KA", 2), S.bufs("QA", 2), S.bufs("VA", 2), S.bufs("VB", 2)
        mk = cx.sb([128, 2, 8, CH], BF16, "mk")
        b_mk = S.buf("mk")
        identf = cx.sb([128, 128], F32, "identf")
        b_identf = S.buf("identf")
        S.dma("sp", identf[:], ident_d, writes=[b_identf], key="idf")
        core = AttnCore(cx, identf, b_identf, n_st=3, npt=6)
        ident = cx.sb([128, 128], BF16, "ident")
        for sd in range(2):
            S.dma("pool", mk[:, sd], masks_d[sd].rearrange("z p q -> p z q"), writes=[b_mk], key="mk")
        S.dma("pool", ident[:], ident_d, writes=[b_mk], key="mk")
        for i in range(2):
            S.op("pool", (lambda i: lambda e: e.memset(VA[i][:, :, 128:129], 1.0))(i), writes=[b_VA[i]])
            S.op("pool", (lambda i: lambda e: e.memset(VB[i][:, :, 64:65], 1.0))(i), writes=[b_VB[i]])
        lam_t = cx.sb([128, 4, 64], F32, "lam_t")
        lam_p = cx.sb([128, 2, 64], F32, "lam_p")
        lam_s = cx.sb([128, 2], F32, "lam_s")
        lam_e = cx.sb([128, 2], F32, "lam_e")
        nlam = cx.sb([128, 1], F32, "nlam")
        gsub = cx.sb([128, 128], F32, "gsub")
        b_lam = S.buf("lam")
        S.dma("sp", lam_t[:].rearrange("p f d -> p (f d)"), lam_d.partition_broadcast(128), writes=[b_lam], key="cst")
        S.dma("sp", gsub[:], subg_d.partition_broadcast(128), writes=[b_lam], key="cst")
        S.op("dve", lambda e: e.tensor_tensor(out=lam_p[:, 0, :], in0=lam_t[:, 0, :], in1=lam_t[:, 1, :], op=ALU.mult),
             reads=[b_lam], writes=[b_lam])
        S.op("dve", lambda e: e.tensor_tensor(out=lam_p[:, 1, :], in0=lam_t[:, 2, :], in1=lam_t[:, 3, :], op=ALU.mult),
             reads=[b_lam], writes=[b_lam])
        S.op("dve", lambda e: e.reduce_sum(out=lam_s[:], in_=lam_p[:], axis=AX.X), reads=[b_lam], writes=[b_lam])
        S.op("act", lambda e: e.activation(out=lam_e[:], in_=lam_s[:], func=AF.Exp), reads=[b_lam], writes=[b_lam])
        S.op("dve", lambda e: e.tensor_tensor(out=nlam[:], in0=lam_e[:, 1:2], in1=lam_e[:, 0:1], op=ALU.subtract),
             reads=[b_lam], writes=[b_lam])
        S.op("dve", lambda e: e.tensor_scalar(out=nlam[:], in0=nlam[:], scalar1=-LAMBDA_INIT0, scalar2=None, op0=ALU.add),
             reads=[b_lam], writes=[b_lam])
        S.op("dve", lambda e: e.tensor_scalar(out=gsub[:], in0=gsub[:], scalar1=1.0 - LAMBDA_INIT0, scalar2=None, op0=ALU.mult),
             reads=[b_lam], writes=[b_lam])

        rsel = cx.sb([8, 2], F32, "rsel")
        ones = cx.sb([8, PC], F32, "ones")
        nl = [cx.sb([8, PC], F32, f"nl{i}") for i in range(2)]
        cp = [cx.sb([8, PC], F32, f"cp{i}") for i in range(2)]
        cqf = cx.sb([8, CH], F32, "cqf")
        r1 = cx.sb([8, PC], F32, "r1")
        tri = [cx.sb([8, 3, PC], BF16, f"tri{i}") for i in range(1)] * 2
        triq = None
        b_c = S.buf("c")
        b_nl, b_cp, b_tri, b_triq = S.bufs("nl", 2), S.bufs("cp", 2), S.bufs("tri", 1) * 2, None
        b_cK, b_cQ = S.buf("cK"), S.buf("cQ")
        S.dma("sp", rsel[:], rsel_d, writes=[b_c], key="cst2")
        S.op("dve", lambda e: e.memset(ones[:], 1.0), writes=[b_c])

        def split3(src, dst, n, bsrc, bdst):
            S.op("dve", lambda e: e.tensor_copy(out=dst[:, 0, :], in_=src), reads=[bsrc], writes=[bdst])
            S.op("dve", lambda e: e.tensor_tensor(out=r1[:, 0:n], in0=src, in1=dst[:, 0, :], op=ALU.subtract),
                 reads=[bsrc, bdst], writes=[b_c])
            S.op("dve", lambda e: e.tensor_copy(out=dst[:, 1, :], in_=r1[:, 0:n]), reads=[b_c], writes=[bdst])
            S.op("dve", lambda e: e.tensor_tensor(out=r1[:, 0:n], in0=r1[:, 0:n], in1=dst[:, 1, :], op=ALU.subtract),
                 reads=[b_c, bdst], writes=[b_c])
            S.op("dve", lambda e: e.tensor_copy(out=dst[:, 2, :], in_=r1[:, 0:n]), reads=[b_c], writes=[bdst])

        gl = cx.sb([8, PC], F32, "gl")
        lc = cx.sb([8, PC], F32, "lc")
        triq2 = [cx.sb([8, 3, PC], BF16, f"triq2_{i}") for i in range(1)] * 2
        b_triq2 = S.bufs("triq2", 1) * 2

        def blend(out, A, B, ca, cb, rd, wr):
            S.op("dve", lambda e: e.tensor_scalar(out=cqf[:], in0=A, scalar1=rsel[:, ca:ca + 1], scalar2=None, op0=ALU.mult),
                 reads=rd + [b_c], writes=[b_c])
            S.op("dve", lambda e: e.scalar_tensor_tensor(out=out, in0=B, scalar=rsel[:, cb:cb + 1], in1=cqf[:], op0=ALU.mult, op1=ALU.add),
                 reads=rd + [b_c], writes=wr)

        for p in range(NP):
            s = p % 2
            S.dma("sp", nl[s][:, 0:CH], nlogf[:, p * CH:(p + 1) * CH], writes=[b_nl[s]], key=f"nl{s}")
            S.dma("sp", nl[s][:, CH:PC], nlogf[:, SL // 2 + p * CH:SL // 2 + (p + 1) * CH], writes=[b_nl[s]], key=f"nl{s}")
            blend(gl[:, 0:CH], nl[s][:, 0:CH], nl[s][:, CH:PC], 0, 1, [b_nl[s]], [b_c])
            blend(gl[:, CH:PC], nl[s][:, 0:CH], nl[s][:, CH:PC], 1, 0, [b_nl[s]], [b_c])
            init = 0.0 if p == 0 else cp[1 - s][:, PC - 1:PC]
            S.op("dve", (lambda s, init: lambda e: e.tensor_tensor_scan(out=cp[s][:], data0=ones[:], data1=gl[:], initial=init,
                                                                     op0=ALU.mult, op1=ALU.add))(s, init),
                 reads=[b_c, b_cp[1 - s]], writes=[b_cp[s]])
            blend(lc[:, 0:CH], cp[s][:, 0:CH], cp[s][:, CH:PC], 0, 1, [b_cp[s]], [b_c])
            blend(lc[:, CH:PC], cp[s][:, 0:CH], cp[s][:, CH:PC], 1, 0, [b_cp[s]], [b_c])
            S.op("dve", (lambda s: lambda e: e.tensor_scalar(out=nl[s][:], in0=lc[:], scalar1=-1.0, scalar2=None, op0=ALU.mult))(s),
                 reads=[b_c], writes=[b_nl[s]])
            split3(nl[s][:], triq2[s], PC, b_nl[s], b_triq2[s])
            S.dma("sp", cQ[:, :, p * CH:(p + 1) * CH].rearrange("t h n -> h t n"), triq2[s][:, :, 0:CH], reads=[b_triq2[s]], writes=[b_cQ], key=f"cq{s}")
            S.dma("sp", cQ[:, :, SL // 2 + p * CH:SL // 2 + (p + 1) * CH].rearrange("t h n -> h t n"), triq2[s][:, :, CH:PC], reads=[b_triq2[s]], writes=[b_cQ], key=f"cq{s}")
            split3(lc[:], tri[s], PC, b_c, b_tri[s])
            S.dma("sp", cK[:, :, p * CH:(p + 1) * CH].rearrange("t h n -> h t n"), tri[s][:, :, 0:CH], reads=[b_tri[s]], writes=[b_cK], key=f"ck{s}")
            S.dma("sp", cK[:, :, SL // 2 + p * CH:SL // 2 + (p + 1) * CH].rearrange("t h n -> h t n"), tri[s][:, :, CH:PC], reads=[b_tri[s]], writes=[b_cK], key=f"ck{s}")

        o1 = [cx.sb([128, 128], F32, f"o1_{i}") for i in range(4)]
        oa = [cx.sb([128, 128], F32, f"oa{i}") for i in range(2)]
        sq = cx.sb([128, 128], F32, "sqj")
        rec = [cx.sb([128, 4], F32, f"rec{i}") for i in range(4)]
        ost = [cx.sb([128, 128], BF16, f"ost{i}") for i in range(4)]
        b_o1, b_oa, b_rec, b_ost, b_sq = S.bufs("o1", 4), S.bufs("oa", 2), S.bufs("rec", 4), S.bufs("ost", 4), S.buf("sq")
        nost = [0]

        units = [("A", h) for h in range(4)] + [("B", h) for h in range(8)]

        def load_unit(ui):
            kind, h = units[ui]
            s = ui % 2
            if kind == "A":
                S.dma("sp", KA[s][:, :], kT[128 * h:128 * h + 128, :], writes=[b_KA[s]], key=f"K{s}")
                S.dma("sp", QA[s][:, :], qT[128 * h:128 * h + 128, :], writes=[b_QA[s]], key=f"Q{s}")
                S.dma("sp", VA[s][:, :, 0:128], v[:, 128 * h:128 * h + 128].rearrange("(n p) d -> p n d", p=128),
                      writes=[b_VA[s]], key=f"V{s}")
            else:
                if h < 2:
                    S.op("pool", (lambda s: lambda e: e.memset(KA[s][64:70, :], 1.0))(s), writes=[b_KA[s]])
                    S.op("pool", (lambda s: lambda e: e.memset(QA[s][64:70, :], 1.0))(s), writes=[b_QA[s]])
                S.dma("sp", KA[s][0:64, :], kT[512 + 64 * h:512 + 64 * h + 64, :], writes=[b_KA[s]], key=f"K{s}")
                S.dma("sp", KA[s][67:70, :], cK[:, h, :], reads=[b_cK], writes=[b_KA[s]], key=f"K{s}")
                S.dma("sp", QA[s][0:64, :], qT[512 + 64 * h:512 + 64 * h + 64, :], writes=[b_QA[s]], key=f"Q{s}")
                S.dma("sp", QA[s][64:67, :], cQ[:, h, :], reads=[b_cQ], writes=[b_QA[s]], key=f"Q{s}")
                S.dma("sp", VB[s][:, :, 0:64], v[:, 512 + 64 * h:512 + 64 * h + 64].rearrange("(n p) d -> p n d", p=128),
                      writes=[b_VB[s]], key=f"VB{s}")

        def epi_A0(j, h):
            def f(views, qss):
                for qs in qss:
                    acc, bacc = views[qs]
                    S.op("dve", (lambda qs, acc: lambda e: e.reciprocal(out=rec[qs][:, 0:1], in_=acc[:, 128:129]))(qs, acc),
                         reads=[bacc], writes=[b_rec[qs]])
                    S.op("dve", (lambda qs, acc: lambda e: e.tensor_scalar(out=o1[qs][:], in0=acc[:, 0:128], scalar1=rec[qs][:, 0:1],
                                                                         scalar2=None, op0=ALU.mult))(qs, acc),
                         reads=[bacc, b_rec[qs]], writes=[b_o1[qs]])
            return f

        def epi_A1(j, h):
            def f(views, qss):
                for qs in qss:
                    acc, bacc = views[qs]
                    a = qs % 2
                    so = nost[0] % 4
                    nost[0] += 1
                    r0 = j * CH + qs * 128
                    S.op("dve", (lambda qs, acc: lambda e: e.reciprocal(out=rec[qs][:, 1:2], in_=acc[:, 128:129]))(qs, acc),
                         reads=[bacc], writes=[b_rec[qs]])
                    S.op("dve", (lambda qs: lambda e: e.tensor_tensor(out=rec[qs][:, 1:2], in0=rec[qs][:, 1:2], in1=nlam[:], op=ALU.mult))(qs),
                         reads=[b_rec[qs], b_lam], writes=[b_rec[qs]])
                    S.op("dve", (lambda qs, acc, a: lambda e: e.scalar_tensor_tensor(out=oa[a][:], in0=acc[:, 0:128], scalar=rec[qs][:, 1:2],
                                                                                   in1=o1[qs][:], op0=ALU.mult, op1=ALU.add))(qs, acc, a),
                         reads=[bacc, b_rec[qs], b_o1[qs]], writes=[b_oa[a]])
                    S.op("act", (lambda qs, a: lambda e: e.activation(out=sq[:], in_=oa[a][:], func=AF.Square, accum_out=rec[qs][:, 2:3]))(qs, a),
                         reads=[b_oa[a]], writes=[b_sq, b_rec[qs]])
                    S.op("act", (lambda qs: lambda e: e.activation(out=rec[qs][:, 3:4], in_=rec[qs][:, 2:3], func=AF.Sqrt, scale=1.0 / 128, bias=EPS))(qs),
                         reads=[b_rec[qs]], writes=[b_rec[qs]])
                    S.op("dve", (lambda qs: lambda e: e.reciprocal(out=rec[qs][:, 3:4], in_=rec[qs][:, 3:4]))(qs),
                         reads=[b_rec[qs]], writes=[b_rec[qs]])
                    S.op("dve", (lambda qs, a, so: lambda e: e.scalar_tensor_tensor(out=ost[so][:], in0=oa[a][:], scalar=rec[qs][:, 3:4], in1=gsub[:],
                                                                                  op0=ALU.mult, op1=ALU.mult))(qs, a, so),
                         reads=[b_oa[a], b_rec[qs], b_lam], writes=[b_ost[so]])
                    S.dma("sp", mixed[r0:r0 + 128, 128 * h:128 * h + 128], ost[so][:], reads=[b_ost[so]], key=f"o{so}")
            return f

        def epi_B(j, h):
            def f(views, qss):
                for qs in qss:
                    acc, bacc = views[qs]
                    so = nost[0] % 4
                    nost[0] += 1
                    r0 = j * CH + qs * 128
                    S.op("dve", (lambda qs, acc: lambda e: e.reciprocal(out=rec[qs][:, 0:1], in_=acc[:, 64:65]))(qs, acc),
                         reads=[bacc], writes=[b_rec[qs]])
                    S.op("dve", (lambda qs, acc, so: lambda e: e.tensor_scalar(out=ost[so][:, 0:64], in0=acc[:, 0:64], scalar1=rec[qs][:, 0:1],
                                                                             scalar2=None, op0=ALU.mult))(qs, acc, so),
                         reads=[bacc, b_rec[qs]], writes=[b_ost[so]])
                    S.dma("sp", mixed[r0:r0 + 128, 512 + 64 * h:512 + 64 * h + 64], ost[so][:, 0:64], reads=[b_ost[so]], key=f"o{so}")
            return f

        Qz = [[cx.sb([128, CH], BF16, f"Qz{a}_{c}") for c in range(2)] for a in range(2)]
        b_Qz = [S.bufs(f"Qz{a}_", 2) for a in range(2)]
        for a in range(2):
            for c in range(2):
                S.op("pool", (lambda a, c: lambda e: e.memset(Qz[a][c][:], 0.0))(a, c), writes=[b_Qz[a][c]])
        nqz = 0
        load_unit(0)
        for ui, (kind, h) in enumerate(units):
            s = ui % 2
            core.flush()
            core.set_mode(kind == "B")
            if ui + 1 < len(units):
                load_unit(ui + 1)
            for side in range(2):
              for j in range(NP):
                lq = side * NP + j
                nk = 8 * (j + 1)
                maps = (0, 1) if kind == "A" else (0,)
                if kind == "A":
                    za = nqz % 2
                    nqz += 1
                    for c in range(2):
                        S.op("pool", (lambda za, c, s, lq: lambda e: e.tensor_copy(
                            out=Qz[za][c][c * 64:(c + 1) * 64, :], in_=QA[s][c * 64:(c + 1) * 64, lq * CH:(lq + 1) * CH]))(za, c, s, lq),
                            reads=[b_QA[s]], writes=[b_Qz[za][c]])
                for c in maps:
                    for kt in range(nk):
                        lt = ktile_local(j, kt, NT)
                        if kind == "A":
                            kap = KA[s][:, lt * 128:(lt + 1) * 128]
                            qap = Qz[za][c][:, :]
                            qzb = b_Qz[za][c]
                            vb = b_VA[s]
                            pv = [(65, VA[s][:, lt, 64:129], 64), (64, VA[s][:, lt, 0:64], 0)]
                            epi = epi_A0(lq, h) if c == 0 else epi_A1(lq, h)
                        else:
                            kap = KA[s][0:70, lt * 128:(lt + 1) * 128]
                            qap = QA[s][0:70, lq * CH:(lq + 1) * CH]
                            qzb = b_QA[s]
                            vb = b_VB[s]
                            pv = [(65, VB[s][:, lt, 0:65], 0)]
                            epi = epi_B(lq, h)
                        blk = dict(k=kap, q=qap, kq_bufs=[b_KA[s], qzb], pv=pv, v_bufs=[vb],
                                   first=(kt == 0), last=(kt == nk - 1), epilogue=epi)
                        if kt >= nk - 8:
                            blk["addmask"] = (ident[:], mk[:, side, kt - (nk - 8), :], [b_mk])
                        core.block(blk)
        core.flush()
        S.emit(final_dma_keys=[f"o{i}" for i in range(4)])
    return nc


def build_mlp(T, final, nc=None, io=None):
    MC = 256
    cx = Ctx(nc, io)
    nc, S = cx.nc, cx.S
    mixed = cx.din("mixed", [T, 1024], BF16)
    xres = cx.din("xres", [T, D])
    wo_d = cx.din("wo", [D, D])
    w1_d = cx.din("w1", [D, 4 * D])
    w2_d = cx.din("w2", [4 * D, D])
    g_d = cx.din("g", [1, D])
    ident_d = cx.din("ident", [128, 128])
    if final:
        gf_d = cx.din("gf", [1, D])
    hout = cx.dout("hout", [T, D])
    with cx.es:
        WO = cx.sb([128, 8, D], BF16, "WO")
        W1 = cx.sb([128, 8, 4 * D], BF16, "W1")
        W2 = cx.sb([128, 32, D], BF16, "W2")
        aT = cx.sb([128, 32, MC], BF16, "aT")
        hb = [cx.sb([128, D], F32, f"hb{i}") for i in range(4)]
        mx = [cx.sb([128, D], BF16, f"mx{i}") for i in range(2)]
        hnT = [cx.sb([128, 8, MC], BF16, f"hnT{i}") for i in range(2)]
        hn = [cx.sb([128, D], BF16, f"hn{i}") for i in range(2)]
        rl = [cx.sb([128, MC], F32, f"rl{i}") for i in range(2)]
        gbc = cx.sb([128, D], F32, "gbc")
        ident = cx.sb([128, 128], BF16, "ident")
        small = [[cx.sb([128, 1], F32, f"sm{i}_{k}") for k in range(3)] for i in range(2)]
        tp = [cx.ps([128, D], BF16, f"tp{i}") for i in range(2)]
        acc = [cx.ps([128, 512], F32, f"acc{i}") for i in range(2)]
        up = [cx.ps([128, 512], F32, f"up{i}") for i in range(3)]
        b_WO, b_W1, b_W2, b_aT, b_g, b_id = S.buf("WO"), S.buf("W1"), S.buf("W2"), S.buf("aT"), S.buf("g"), S.buf("id")
        b_hb, b_mx, b_hnT, b_hn, b_rl = S.bufs("hb", 4), S.bufs("mx", 2), S.bufs("hnT", 2), S.bufs("hn", 2), S.bufs("rl", 2)
        b_small, b_tp, b_acc, b_up = S.bufs("small", 2), S.bufs("tp", 2), S.bufs("acc", 2), S.bufs("up", 3)
        if final:
            gfbc = cx.sb([128, D], F32, "gfbc")
            S.dma("sp", gfbc[:], gf_d.partition_broadcast(128), writes=[b_g], key="g")
        S.dma("sp", gbc[:], g_d.partition_broadcast(128), writes=[b_g], key="g")
        S.dma("pool", ident[:], ident_d, writes=[b_id], key="id")
        wov = wo_d.rearrange("(k p) c -> p k c", p=128)
        for kc in range(8):
            S.dma("pool", WO[:, kc, :], wov[:, kc, :], writes=[b_WO], key="WO")
        w1v = w1_d.rearrange("(k p) c -> p k c", p=128)
        for kc in range(8):
            S.dma("pool", W1[:, kc, :], w1v[:, kc, :], writes=[b_W1], key="W1")
        w2v = w2_d.rearrange("(k p) c -> p k c", p=128)
        for fc in range(0, 32, 4):
            S.dma("pool", W2[:, fc:fc + 4, :], w2v[:, fc:fc + 4, :], writes=[b_W2], key="W2")

        nacc = 0
        nup = 0
        for ci in range(T // MC):
            cp = ci % 2
            for tt in range(2):
                r0 = ci * MC + tt * 128
                hi = cp * 2 + tt
                m = tt
                S.dma("sp", mx[m][:], mixed[r0:r0 + 128, :], writes=[b_mx[m]], key=f"mx{m}")
                S.dma("sp", hb[hi][:], xres[r0:r0 + 128, :], writes=[b_hb[hi]], key=f"hb{hi}")
                for kc in range(8):
                    S.op("pe", (lambda kc, m: lambda e: e.transpose(out=tp[m][:, kc * 128:(kc + 1) * 128],
                                                                   in_=mx[m][:, kc * 128:(kc + 1) * 128], identity=ident[:]))(kc, m),
                         reads=[b_mx[m], b_id], writes=[b_tp[m]])
                S.op("act", (lambda m, cp, tt: lambda e: e.copy(out=hnT[cp][:, :, tt * 128:(tt + 1) * 128],
                                                              in_=tp[m][:].rearrange("p (k t) -> p k t", k=8)))(m, cp, tt),
                     reads=[b_tp[m]], writes=[b_hnT[cp]])
                for half in range(2):
                    a = nacc % 2
                    nacc += 1
                    for kc in range(8):
                        S.op("pe", (lambda kc, a, cp, tt, half: lambda e: e.matmul(
                            acc[a][:], lhsT=hnT[cp][:, kc, tt * 128:(tt + 1) * 128], rhs=WO[:, kc, half * 512:(half + 1) * 512],
                            start=(kc == 0), stop=(kc == 7)))(kc, a, cp, tt, half),
                            reads=[b_hnT[cp], b_WO], writes=[b_acc[a]])
                    S.op("dve", (lambda a, hi, half: lambda e: e.tensor_tensor(
                        out=hb[hi][:, half * 512:(half + 1) * 512], in0=acc[a][:], in1=hb[hi][:, half * 512:(half + 1) * 512], op=ALU.add))(a, hi, half),
                        reads=[b_acc[a], b_hb[hi]], writes=[b_hb[hi]])
            for tt in range(2):
                hi = cp * 2 + tt
                m = tt
                emit_norm_T(cx, hb[hi], b_hb[hi], gbc, ident, hn[m], b_hn[m], tp[m], b_tp[m],
                            hnT[cp][:, :, tt * 128:(tt + 1) * 128], b_hnT[cp], small[m], b_small[m], mx[m], b_mx[m], cb=[b_g, b_id])
            for fb in range(32):
                u = nup % 3
                nup += 1
                r = fb % 2
                for kc in range(8):
                    S.op("pe", (lambda kc, u, fb, cp: lambda e: e.matmul(
                        up[u][:, 0:MC], lhsT=W1[:, kc, fb * 128:(fb + 1) * 128], rhs=hnT[cp][:, kc, :],
                        start=(kc == 0), stop=(kc == 7)))(kc, u, fb, cp),
                        reads=[b_hnT[cp], b_W1], writes=[b_up[u]])
                S.op("act", (lambda u, r: lambda e: e.activation(out=rl[r][:], in_=up[u][:, 0:MC], func=AF.Relu))(u, r),
                     reads=[b_up[u]], writes=[b_rl[r]])
                S.op("pool", (lambda r, fb: lambda e: e.tensor_tensor(out=aT[:, fb, :], in0=rl[r][:], in1=rl[r][:], op=ALU.mult))(r, fb),
                     reads=[b_rl[r]], writes=[b_aT])
            for tt in range(2):
                hi = cp * 2 + tt
                r0 = ci * MC + tt * 128
                for half in range(2):
                    a = nacc % 2
                    nacc += 1
                    for fc in range(32):
                        S.op("pe", (lambda fc, a, tt, half: lambda e: e.matmul(
                            acc[a][:], lhsT=aT[:, fc, tt * 128:(tt + 1) * 128], rhs=W2[:, fc, half * 512:(half + 1) * 512],
                            start=(fc == 0), stop=(fc == 31)))(fc, a, tt, half),
                            reads=[b_aT, b_W2], writes=[b_acc[a]])
                    S.op("dve", (lambda a, hi, half: lambda e: e.tensor_tensor(
                        out=hb[hi][:, half * 512:(half + 1) * 512], in0=acc[a][:], in1=hb[hi][:, half * 512:(half + 1) * 512], op=ALU.add))(a, hi, half),
                        reads=[b_acc[a], b_hb[hi]], writes=[b_hb[hi]])
                if final:
                    m = tt
                    ss, sd, rstd = small[m]
                    S.op("act", (lambda hi, m, ss: lambda e: e.activation(out=mx[m][:], in_=hb[hi][:], func=AF.Square, accum_out=ss[:]))(hi, m, ss),
                         reads=[b_hb[hi]], writes=[b_mx[m], b_small[m]])
                    S.op("act", (lambda ss, sd: lambda e: e.activation(out=sd[:], in_=ss[:], func=AF.Sqrt, scale=1.0 / D, bias=EPS))(ss, sd),
                         reads=[b_small[m]], writes=[b_small[m]])
                    S.op("dve", (lambda sd, rstd: lambda e: e.reciprocal(out=rstd[:], in_=sd[:]))(sd, rstd),
                         reads=[b_small[m]], writes=[b_small[m]])
                    S.op("dve", (lambda hi, rstd: lambda e: e.scalar_tensor_tensor(out=hb[hi][:], in0=hb[hi][:], scalar=rstd[:], in1=gfbc[:],
                                                                                 op0=ALU.mult, op1=ALU.mult))(hi, rstd),
                         reads=[b_hb[hi], b_small[m], b_g], writes=[b_hb[hi]])
                S.dma("sp", hout[r0:r0 + 128, :], hb[hi][:], reads=[b_hb[hi]], key=f"ho{hi}")
        S.emit(final_dma_keys=[f"ho{i}" for i in range(4)])
    return nc


U8 = mybir.dt.uint8
NBIS = 21
BIS0 = 32.0
TOPK = 256


def zone_negmasks_q(r):
    m = np.zeros((128, 4, 1024), np.float32)
    q = np.arange(128)[:, None]
    zk = np.arange(CH)[None, :]
    for qs in range(4):
        m[:, qs, 0:CH] = np.where(zk <= 128 * qs + q, 0.0, -30000.0)
        m[:, qs, CH:] = 0.0 if r == 1 else -30000.0
    return m


def build_dsa(SL, nc=None, io=None):
    T = SL // 2
    nchunk = T // CH
    NP = SL // 1024
    NT = SL // 128
    NKMAX = SL
    cx = Ctx(nc, io)
    nc, S = cx.nc, cx.S
    qT = cx.din("qT", [1024, SL], BF16)
    kT = cx.din("kT", [1024, SL], BF16)
    v = cx.din("v", [SL, 16, 65], BF16)
    iqT = cx.din("iqT", [512, SL], BF16)
    ikT = cx.din("ikT", [64, SL], BF16)
    iw_d = cx.din("iw", [SL, 8])
    nmq_d = cx.din("nmq", [128, 4, 1024])
    ident_d = cx.din("ident", [128, 128])
    mixed = cx.dout("mixed", [T, 1024], BF16)
    with cx.es:
        K2 = [cx.sb([128, NKMAX], BF16, f"K2{i}") for i in range(2)]
        Q2 = [[cx.sb([128, CH], BF16, f"Q2{i}_{hh}") for hh in range(2)] for i in range(2)]
        V2 = [cx.sb([128, NKMAX // 128, 130], BF16, f"V2{i}") for i in range(2)]
        IQ = [cx.sb([128, 4, CH], BF16, f"IQ{i}") for i in range(2)]
        IK = cx.sb([128, SL], BF16, "IK")
        IW = cx.sb([128, T // 128, 8], F32, "IW")
        sc = cx.sb([128, NKMAX], F32, "sc")
        msk = cx.sb([128, NKMAX], BF16, "msk")
        MT = cx.sb([128, NKMAX // 128, CH], U8, "MT")
        nmq = cx.sb([128, 4, 1024], BF16, "nmq")
        ident = cx.sb([128, 128], BF16, "ident")
        rl = [cx.sb([128, CH], F32, f"rl{i}") for i in range(2)]
        bs = [cx.sb([128, 1], F32, f"bs{i}") for i in range(4)]
        rec = [cx.sb([128, 1], F32, f"rec{i}") for i in range(4)]
        ost = [cx.sb([128, 64], BF16, f"ost{i}") for i in range(4)]
        lg0 = cx.ps([128, CH], F32, "lg0")
        tpm = cx.ps([128, 1024], BF16, "tpm")
        b_lg0, b_tpm = S.buf("lg0"), S.buf("tpm")
        identf = cx.sb([128, 128], F32, "identf")
        b_identf = S.buf("identf")
        S.dma("sp", identf[:], ident_d, writes=[b_identf], key="idf")
        core = AttnCore(cx, identf, b_identf, tq=[lg0], b_tq=[b_lg0], tq_stride=65, tq_per_bank=4, npt=6, single_acc=True)
        lgs = [lg0, core.st[0], core.st[1]]
        b_lgs = [b_lg0, core.b_st[0], core.b_st[1]]
        b_K2, b_Q2, b_V2, b_IQ = S.bufs("K2", 2), S.bufs("Q2", 2), S.bufs("V2", 2), S.bufs("IQ", 2)
        b_IK, b_IW, b_sc, b_msk, b_MT, b_cst = S.buf("IK"), S.buf("IW"), S.buf("sc"), S.buf("msk"), S.buf("MT"), S.buf("cst")
        b_rl, b_bs, b_rec, b_ost = S.bufs("rl", 2), S.buf("bs"), S.bufs("rec", 4), S.bufs("ost", 4)

        for i in range(2):
            for hh in range(2):
                S.op("pool", (lambda i, hh: lambda e: e.memset(Q2[i][hh][:], 0.0))(i, hh), writes=[b_Q2[i]])
        S.dma("pool", nmq[:], nmq_d, writes=[b_cst], key="cst")
        S.dma("pool", ident[:], ident_d, writes=[b_cst], key="cst")
        S.dma("sp", IK[0:64, :], ikT, writes=[b_IK], key="IK")
        S.dma("sp", IK[64:128, :], ikT, writes=[b_IK], key="IK")
        S.dma("sp", IW[:], iw_d[0:T, :].rearrange("(n p) e -> p n e", p=128), writes=[b_IW], key="IW")
        lo, mid, cnt, flag = bs
        nlg = 0
        nrl = 0
        nost = 0
        nmm = 0
        nunit = 0

        def load_unit(j, hp, s):
            n1 = CH * (j + 1)
            for off in (0, SL // 2):
                S.dma("sp", K2[s][:, off:off + n1], kT[hp * 128:(hp + 1) * 128, off:off + n1], writes=[b_K2[s]], key=f"K{s}")
                S.dma("sp", V2[s][:, off // 128:(off + n1) // 128, :],
                      v[off:off + n1, 2 * hp:2 * hp + 2, :].rearrange("(n p) h d -> p n (h d)", p=128),
                      writes=[b_V2[s]], key=f"V{s}")
            for hh in range(2):
                S.dma("sp", Q2[s][hh][hh * 64:(hh + 1) * 64, :], qT[hp * 128 + hh * 64:hp * 128 + (hh + 1) * 64, j * CH:(j + 1) * CH],
                      writes=[b_Q2[s]], key=f"Q{s}")

        for j in range(nchunk):
            NK = 1024 * (j + 1)
            nkt = NK // 128
            iqs = j % 2
            S.dma("sp", IQ[iqs][:], iqT[:, j * CH:(j + 1) * CH].rearrange("(h p) t -> p h t", p=128), writes=[b_IQ[iqs]], key=f"IQ{iqs}")
            load_unit(j, 0, nunit % 2)
            for qs in range(4):
                ti = 4 * j + qs
                for kc in range(NK // CH):
                    lch = kc if kc < j else (NP + kc - j if kc < 2 * j else (j if kc == 2 * j else NP + j))
                    for h in range(8):
                        hp, hh = h // 2, h % 2
                        li = nlg % 3
                        nlg += 1
                        ri = nrl % 2
                        nrl += 1
                        S.op("pe", (lambda li, hp, hh, lch, qs, iqs: lambda e: e.matmul(
                            lgs[li][:], lhsT=IQ[iqs][hh * 64:(hh + 1) * 64, hp, qs * 128:(qs + 1) * 128],
                            rhs=IK[hh * 64:(hh + 1) * 64, lch * CH:(lch + 1) * CH], start=True, stop=True))(li, hp, hh, lch, qs, iqs),
                            reads=[b_IQ[iqs], b_IK], writes=[b_lgs[li]])
                        S.op("act", (lambda li, ri: lambda e: e.activation(out=rl[ri][:], in_=lgs[li][:], func=AF.Relu))(li, ri),
                             reads=[b_lgs[li]], writes=[b_rl[ri]])
                        if h == 0:
                            S.op("dve", (lambda ri, kc, ti: lambda e: e.tensor_scalar(
                                out=sc[:, kc * CH:(kc + 1) * CH], in0=rl[ri][:], scalar1=IW[:, ti, 0:1], scalar2=None,
                                op0=ALU.mult))(ri, kc, ti),
                                reads=[b_rl[ri], b_IW], writes=[b_sc])
                        else:
                            S.op("dve", (lambda ri, kc, ti, h: lambda e: e.scalar_tensor_tensor(
                                out=sc[:, kc * CH:(kc + 1) * CH], in0=rl[ri][:], scalar=IW[:, ti, h:h + 1], in1=sc[:, kc * CH:(kc + 1) * CH],
                                op0=ALU.mult, op1=ALU.add))(ri, kc, ti, h),
                                reads=[b_rl[ri], b_IW, b_sc], writes=[b_sc])
                S.op("dve", (lambda NK, qs: lambda e: e.tensor_tensor(out=sc[:, NK - 1024:NK], in0=sc[:, NK - 1024:NK], in1=nmq[:, qs, :], op=ALU.add))(NK, qs),
                     reads=[b_sc, b_cst], writes=[b_sc])
                S.op("dve", lambda e: e.memset(lo[:], -BIS0), writes=[b_bs])
                for it in range(NBIS):
                    step = BIS0 / (2 ** it)
                    S.op("dve", (lambda step: lambda e: e.tensor_scalar(out=mid[:], in0=lo[:], scalar1=step, scalar2=None, op0=ALU.add))(step),
                         reads=[b_bs], writes=[b_bs])
                    S.op("dve", (lambda NK: lambda e: e.tensor_scalar(out=msk[:, 0:NK], in0=sc[:, 0:NK], scalar1=mid[:], scalar2=None,
                                                                     op0=ALU.is_ge, op1=ALU.add, accum_out=cnt[:]))(NK),
                         reads=[b_sc, b_bs], writes=[b_msk, b_bs])
                    S.op("dve", (lambda step: lambda e: e.tensor_scalar(out=flag[:], in0=cnt[:], scalar1=TOPK - 0.5, scalar2=step,
                                                                       op0=ALU.is_ge, op1=ALU.mult))(step),
                         reads=[b_bs], writes=[b_bs])
                    S.op("dve", lambda e: e.tensor_tensor(out=lo[:], in0=lo[:], in1=flag[:], op=ALU.add), reads=[b_bs], writes=[b_bs])
                S.op("dve", (lambda NK: lambda e: e.tensor_scalar(out=msk[:, 0:NK], in0=sc[:, 0:NK], scalar1=lo[:], scalar2=None, op0=ALU.is_ge))(NK),
                     reads=[b_sc, b_bs], writes=[b_msk])
                for k0 in range(0, nkt, 8):
                    for kk in range(8):
                        S.op("pe", (lambda k0, kk: lambda e: e.transpose(out=tpm[:, kk * 128:(kk + 1) * 128],
                                                                        in_=msk[:, (k0 + kk) * 128:(k0 + kk + 1) * 128], identity=ident[:]))(k0, kk),
                             reads=[b_msk, b_cst], writes=[b_tpm])
                    S.op("act", (lambda k0, qs: lambda e: e.copy(out=MT[:, k0:k0 + 8, qs * 128:(qs + 1) * 128],
                                                                in_=tpm[:].rearrange("p (k t) -> p k t", k=8)))(k0, qs),
                         reads=[b_tpm], writes=[b_MT])
            for hp in range(8):
                s = nunit % 2
                nunit += 1
                core.flush()
                if hp + 1 < 8:
                    load_unit(j, hp + 1, nunit % 2)
                for hh in range(2):
                    head = 2 * hp + hh

                    def epi(views, qss, j=j, head=head):
                        nonlocal nost
                        for qs in qss:
                            acc, bacc = views[qs]
                            so = nost % 4
                            nost += 1
                            r0 = j * CH + qs * 128
                            S.op("dve", (lambda qs, acc: lambda e: e.reciprocal(out=rec[qs][:], in_=acc[:, 64:65]))(qs, acc),
                                 reads=[bacc], writes=[b_rec[qs]])
                            S.op("dve", (lambda qs, acc, so: lambda e: e.tensor_scalar(out=ost[so][:], in0=acc[:, 0:64], scalar1=rec[qs][:],
                                                                                     scalar2=None, op0=ALU.mult))(qs, acc, so),
                                 reads=[bacc, b_rec[qs]], writes=[b_ost[so]])
                            S.dma("sp", mixed[r0:r0 + 128, head * 64:(head + 1) * 64], ost[so][:], reads=[b_ost[so]], key=f"o{so}")

                    for kt in range(nkt):
                        meng = "pool" if (nmm % 4 == 0) else "dve"
                        nmm += 1
                        lt = ktile_local(j, kt, NT)
                        blk = dict(k=K2[s][:, lt * 128:(lt + 1) * 128], q=Q2[s][hh][:, :],
                                   kq_bufs=[b_K2[s], b_Q2[s]], pv=[(65, V2[s][:, lt, hh * 65:(hh + 1) * 65], 0)], v_bufs=[b_V2[s]],
                                   first=(kt == 0), last=(kt == nkt - 1), epilogue=epi,
                                   masks=[(meng, MT[:, kt, :], [b_MT])])
                        core.block(blk)
            core.flush()
        S.emit(final_dma_keys=[f"o{i}" for i in range(4)])
    return nc


def build_fused(SL):
    T = SL // 2
    nc = bass.Bass("TRN2", target_bir_lowering=False)

    def ein(name, shape, dt=F32):
        return nc.dram_tensor(name, list(shape), dt, kind="ExternalInput").ap()

    def scr(name, shape, dt):
        return nc.dram_tensor(name, list(shape), dt, kind="Internal").ap()

    E = dict(
        x=ein("x", [SL, D]), w_in0=ein("w_in0", [D, 3080]), w_in1=ein("w_in1", [D, 3656]),
        wo0=ein("wo0", [D, D]), wo1=ein("wo1", [D, D]),
        w1_0=ein("w1_0", [D, 4 * D]), w1_1=ein("w1_1", [D, 4 * D]), w2_0=ein("w2_0", [4 * D, D]), w2_1=ein("w2_1", [4 * D, D]),
        g_mix0=ein("g_mix0", [1, D]), g_mix1=ein("g_mix1", [1, D]), g_mlp0=ein("g_mlp0", [1, D]), g_mlp1=ein("g_mlp1", [1, D]),
        g_final=ein("g_final", [1, D]), bf=ein("bf", [8, 1]), lam=ein("lam", [1, 256]), subg=ein("subg", [1, 128]),
        lng=ein("lng", [1, 64]), lnb=ein("lnb", [1, 64]), ident=ein("ident", [128, 128]),
        cq=ein("cq", [128, SL]), sq=ein("sq", [128, SL]), ck=ein("ck", [128, SL]), sk=ein("sk", [128, SL]),
        ctk=ein("ctk", [SL, 8]), stk=ein("stk", [SL, 8]),
        masks0=ein("masks0", [2, 8, 128, CH]), nmq=ein("nmq", [128, 4, 1024]), rsel=ein("rsel", [8, 2]),
    )
    out = nc.dram_tensor("out", [T, D], F32, kind="ExternalOutput").ap()
    qT0, kT0 = scr("qT0", [1024, SL], BF16), scr("kT0", [1024, SL], BF16)
    v0, nlogf = scr("v0", [SL, 1024], BF16), scr("nlogf", [8, SL], F32)
    mixed0, h1 = scr("mixed0", [SL, 1024], BF16), scr("h1", [SL, D], F32)
    qT1, kT1 = scr("qT1", [1024, SL], BF16), scr("kT1", [1024, SL], BF16)
    v1 = scr("v1", [SL, 16, 65], BF16)
    iqT, ikT, iw = scr("iqT", [512, SL], BF16), scr("ikT", [64, SL], BF16), scr("iw", [SL, 8], F32)
    mixed1 = scr("mixed1", [T, 1024], BF16)
    rope = dict(cq=E["cq"], sq=E["sq"], ck=E["ck"], sk=E["sk"], ident=E["ident"])
    build_inproj(0, SL, nc, dict(rope, x=E["x"], w=E["w_in0"], g=E["g_mix0"], bf=E["bf"], qT=qT0, kT=kT0, v=v0, nlogf=nlogf))
    build_attn0(SL, nc, dict(qT=qT0, kT=kT0, v=v0, nlogf=nlogf, masks=E["masks0"], ident=E["ident"], rsel=E["rsel"],
                             lam=E["lam"], subg=E["subg"], mixed=mixed0))
    build_mlp(SL, False, nc, dict(mixed=mixed0, xres=E["x"], wo=E["wo0"], w1=E["w1_0"], w2=E["w2_0"], g=E["g_mlp0"],
                                  ident=E["ident"], hout=h1))
    build_inproj(1, SL, nc, dict(rope, x=h1, w=E["w_in1"], g=E["g_mix1"], lng=E["lng"], lnb=E["lnb"], ctk=E["ctk"], stk=E["stk"],
                                 qT=qT1, kT=kT1, v=v1, iqT=iqT, ikT=ikT, iw=iw))
    build_dsa(SL, nc, dict(qT=qT1, kT=kT1, v=v1, iqT=iqT, ikT=ikT, iw=iw, nmq=E["nmq"], ident=E["ident"], mixed=mixed1))
    build_mlp(T, True, nc, dict(mixed=mixed1, xres=h1[0:T, :], wo=E["wo1"], w1=E["w1_1"], w2=E["w2_1"], g=E["g_mlp1"],
                                gf=E["g_final"], ident=E["ident"], hout=out))
    return nc


def local_positions(SL, r):
    return np.concatenate([own_positions(SL, r), own_positions(SL, 1 - r)])


_PROG = {}


def make_in_maps(x, norm_mix, w_in_even, b_forget, lambda_q1, lambda_k1, lambda_q2, lambda_k2,
                 diff_subln_g, w_out_even, w_in_odd, idx_ln_g, idx_ln_b, w_out_odd,
                 norm_mlp, w_mlp_in, w_mlp_out, norm_final, ncores=NCORE):
    f32 = np.float32
    x = np.asarray(x, f32)
    B, SL, _ = x.shape
    A = lambda a: np.ascontiguousarray(np.asarray(a, f32))
    shared = dict(
        w_in0=A(w_in_even[0]), w_in1=A(w_in_odd[0]), wo0=A(w_out_even[0]), wo1=A(w_out_odd[0]),
        w1_0=A(w_mlp_in[0]), w1_1=A(w_mlp_in[1]), w2_0=A(w_mlp_out[0]), w2_1=A(w_mlp_out[1]),
        g_mix0=A(norm_mix[0:1]), g_mix1=A(norm_mix[1:2]), g_mlp0=A(norm_mlp[0:1]), g_mlp1=A(norm_mlp[1:2]),
        g_final=A(norm_final).reshape(1, D), bf=A(b_forget[0]).reshape(8, 1),
        lam=np.concatenate([A(a[0]) for a in (lambda_q1, lambda_k1, lambda_q2, lambda_k2)]).reshape(1, 256),
        subg=A(diff_subln_g[0:1]), lng=A(idx_ln_g[0:1]), lnb=A(idx_ln_b[0:1]), ident=np.eye(128, dtype=f32),
    )
    per_r = []
    for r in range(2):
        pos = local_positions(SL, r)
        cq, sq = rope_tables_fm(pos, 0.125)
        ck, sk = rope_tables_fm(pos, 1.0)
        per_r.append(dict(cq=cq, sq=sq, ck=ck, sk=sk, ctk=np.ascontiguousarray(ck[0:8].T), stk=np.ascontiguousarray(sk[0:8].T),
                          masks0=zone_negmasks_local(r), nmq=zone_negmasks_q(r),
                          rsel=np.tile(np.array([[1.0 - r, r]], f32), (8, 1)), pos=pos))
    maps = []
    for c in range(ncores):
        b, r = c // 2, c % 2
        m = dict(shared)
        m.update({k: v for k, v in per_r[r].items() if k != "pos"})
        m["x"] = np.ascontiguousarray(x[b][per_r[r]["pos"]])
        maps.append(m)
    return maps


def kernel(**inputs):
    x = np.asarray(inputs["x"], np.float32)
    B, SL, _ = x.shape
    T = SL // 2
    if ("F", SL) not in _PROG:
        _PROG[("F", SL)] = build_fused(SL)
    nc = _PROG[("F", SL)]
    maps = make_in_maps(**inputs)
    res = run_bass_kernel_spmd(nc, maps, core_ids=list(range(NCORE))).results
    out = np.zeros((B, SL, D), np.float32)
    for c in range(NCORE):
        out[c // 2][own_positions(SL, c % 2)] = np.asarray(res[c]["out"])
    return out
```

```python
import contextlib
import numpy as np
import ml_dtypes
import concourse.bass as bass
import concourse.mybir as mybir
from concourse.bass_utils import run_bass_kernel_spmd

F32 = mybir.dt.float32
BF16 = mybir.dt.bfloat16
AF = mybir.ActivationFunctionType
ALU = mybir.AluOpType
AX = mybir.AxisListType

D = 1024
HD = 64
EPS = 1e-6
CH = 512
NCORE = 8


class Buf:
    __slots__ = ("name", "w", "readers")

    def __init__(self, name):
        self.name = name
        self.w = None
        self.readers = []


class Op:
    __slots__ = ("eng", "fn", "deps", "is_dma", "key", "val", "milestone", "idx", "dmaval")

    def __init__(self, eng, fn, is_dma=False, key=None):
        self.eng = eng
        self.fn = fn
        self.deps = []
        self.is_dma = is_dma
        self.key = key
        self.val = 0
        self.milestone = False
        self.dmaval = 0


class Sched:
    ENGS = ("pe", "act", "dve", "pool", "sp")

    def __init__(self, nc, multi=False):
        self.nc = nc
        self.multi = multi
        self.ops = []
        self.dma_cnt = {}

    def buf(self, name):
        return Buf(name)

    def bufs(self, name, n):
        return [Buf(f"{name}{i}") for i in range(n)]

    def _record(self, op, reads, writes):
        deps = []
        for b in reads:
            if b.w is not None:
                deps.append(b.w)
        for b in writes:
            if b.w is not None:
                deps.append(b.w)
            deps.extend(b.readers)
        for b in reads:
            b.readers.append(op)
            if len(b.readers) > 64:
                seen = {}
                for r in b.readers:
                    seen[(r.eng, r.key)] = r
                b.readers = list(seen.values())
        for b in writes:
            b.w = op
            b.readers = []
        uniq = []
        seen = set()
        for d in deps:
            if id(d) in seen or d is op:
                continue
            seen.add(id(d))
            if d.eng == "pe" and op.eng == "pe" and not d.is_dma and not op.is_dma:
                continue
            if d.is_dma and op.is_dma and d.key == op.key:
                continue
            uniq.append(d)
        for d in uniq:
            if d.is_dma:
                op.deps.append((d, self.dma_cnt[d.key]))
            else:
                d.milestone = True
                op.deps.append((d, None))
        self.ops.append(op)
        return op

    def op(self, eng, fn, reads=(), writes=()):
        return self._record(Op(eng, fn), reads, writes)

    def dma(self, eng, out, in_, reads=(), writes=(), key=None, **kw):
        assert key is not None
        o = Op(eng, lambda e: e.dma_start(out=out, in_=in_, **kw), is_dma=True, key=key)
        self.dma_cnt.setdefault(key, 0)
        self._record(o, reads, writes)
        self.dma_cnt[key] += 1
        o.dmaval = self.dma_cnt[key]
        return o

    def emit(self, final_dma_keys=()):
        nc = self.nc
        Sched._ph = getattr(Sched, "_ph", 0) + 1
        ph = Sched._ph
        handles = []
        with contextlib.ExitStack() as es:
            if self.multi:
                esem = {e: nc.alloc_semaphore(name=f"c{ph}_{e}") for e in self.ENGS}
                dsem = {k: nc.alloc_semaphore(name=f"d{ph}_{k}") for k in self.dma_cnt}
                handles = list(esem.values()) + list(dsem.values())
            else:
                esem = {e: es.enter_context(nc.semaphore(f"c_{e}")) for e in self.ENGS}
                dsem = {k: es.enter_context(nc.semaphore(f"d_{k}")) for k in self.dma_cnt}
            cnt = {e: 0 for e in self.ENGS}
            for o in self.ops:
                if not o.is_dma and o.milestone:
                    cnt[o.eng] += 1
                    o.val = cnt[o.eng]
            streams = {e: [] for e in self.ENGS}
            for o in self.ops:
                streams[o.eng].append(o)
            block = es.enter_context(nc.Block())

            def run(engname, eng):
                known = {}
                for o in streams[engname]:
                    need = {}
                    for d, v in o.deps:
                        if d.is_dma:
                            t = ("d", d.key)
                            val = 16 * v
                        else:
                            t = ("e", d.eng)
                            val = d.val
                        if need.get(t, 0) < val:
                            need[t] = val
                    for t, val in need.items():
                        if known.get(t, 0) >= val:
                            continue
                        known[t] = val
                        sem = dsem[t[1]] if t[0] == "d" else esem[t[1]]
                        eng.wait_ge(sem, val)
                    ins = o.fn(eng)
                    if o.is_dma:
                        ins.then_inc(dsem[o.key], 16)
                    elif o.milestone:
                        ins.then_inc(esem[o.eng], 1)
                if engname == "sp":
                    for k in final_dma_keys:
                        eng.wait_ge(dsem[k], 16 * self.dma_cnt[k])

            @block.tensor
            def _(e):
                run("pe", e)

            @block.scalar
            def _(e):
                run("act", e)

            @block.vector
            def _(e):
                run("dve", e)

            @block.gpsimd
            def _(e):
                run("pool", e)

            @block.sync
            def _(e):
                run("sp", e)
        if self.multi:
            nc.clear_and_free_semaphores(handles)
            nc.all_engine_barrier()


def rope_tables_fm(pos, qscale):
    rot = HD // 4
    half = rot // 2
    inv = (500000.0 ** (-np.arange(0, rot, 2, dtype=np.float32) / np.float32(rot))).astype(np.float32)
    ang = pos.astype(np.float32)[None, :] * inv[:, None]
    cos = np.cos(ang).astype(np.float32)
    sin = np.sin(ang).astype(np.float32)
    T = pos.shape[0]
    C = np.ones((128, T), np.float32)
    S = np.zeros((128, T), np.float32)
    for base in (0, 64):
        C[base:base + half] = cos
        C[base + half:base + rot] = cos
        S[base:base + half] = sin
        S[base + half:base + rot] = sin
    return (C * np.float32(qscale)).astype(np.float32), (S * np.float32(qscale)).astype(np.float32)


def own_positions(S, r):
    nch = S // CH
    pos = []
    for j in range(nch // 2):
        g = 2 * j + r
        pos.append(np.arange(g * CH, (g + 1) * CH))
    return np.concatenate(pos)


class Ctx:
    _uid = [0]

    def __init__(self, nc=None, io=None):
        self.fused = nc is not None
        self.nc = nc if nc is not None else bass.Bass("TRN2", target_bir_lowering=False)
        self.S = Sched(self.nc, multi=self.fused)
        self.es = contextlib.ExitStack()
        self.n = 0
        self.io = io or {}
        Ctx._uid[0] += 1
        self.uid = Ctx._uid[0]

    def sb(self, shape, dt, name=None):
        self.n += 1
        return self.es.enter_context(self.nc.sbuf_tensor(f"s{self.uid}_{name}_{self.n}", list(shape), dt))

    def ps(self, shape, dt, name=None):
        self.n += 1
        return self.es.enter_context(self.nc.psum_tensor(f"p{self.uid}_{name}_{self.n}", list(shape), dt))

    def din(self, name, shape, dt=F32):
        if name in self.io:
            ap = self.io[name]
            assert list(ap.shape) == list(shape), (name, ap.shape, shape)
            return ap
        assert not self.fused, name
        return self.nc.dram_tensor(name, list(shape), dt, kind="ExternalInput").ap()

    def dout(self, name, shape, dt=F32):
        if name in self.io:
            ap = self.io[name]
            assert list(ap.shape) == list(shape), (name, ap.shape, shape)
            return ap
        assert not self.fused, name
        return self.nc.dram_tensor(name, list(shape), dt, kind="ExternalOutput").ap()

    def scratch(self, name, shape, dt):
        if name in self.io:
            return self.io[name]
        return self.nc.dram_tensor(f"{name}_{self.uid}", list(shape), dt, kind="Internal").ap()


def emit_norm_T(cx, xt, xt_b, g_bc, ident, hn, hn_b, tp, tp_b, hnT_dst, hnT_b, small, small_b, junk, junk_b, cb=()):
    S = cx.S
    ss, sd, rstd = small
    S.op("act", lambda e: e.activation(out=junk[:], in_=xt[:], func=AF.Square, accum_out=ss[:]),
         reads=[xt_b], writes=[junk_b, small_b])
    S.op("act", lambda e: e.activation(out=sd[:], in_=ss[:], func=AF.Sqrt, scale=1.0 / D, bias=EPS),
         reads=[small_b], writes=[small_b])
    S.op("dve", lambda e: e.reciprocal(out=rstd[:], in_=sd[:]), reads=[small_b], writes=[small_b])
    S.op("dve", lambda e: e.scalar_tensor_tensor(out=hn[:], in0=xt[:], scalar=rstd[:], in1=g_bc[:],
                                                 op0=ALU.mult, op1=ALU.mult),
         reads=[xt_b, small_b] + list(cb), writes=[hn_b])
    for kc in range(8):
        S.op("pe", (lambda kc: lambda e: e.transpose(out=tp[:, kc * 128:(kc + 1) * 128],
                                                    in_=hn[:, kc * 128:(kc + 1) * 128], identity=ident[:]))(kc),
             reads=[hn_b] + list(cb), writes=[tp_b])
    S.op("act", lambda e: e.copy(out=hnT_dst, in_=tp[:].rearrange("p (k t) -> p k t", k=8)),
         reads=[tp_b], writes=[hnT_b])


def inproj_spec(layer):
    if layer == 0:
        return dict(
            C=3080,
            rot=[(0, 1024)],
            fm=[(c, True, True, "qT", c) for c in range(0, 512, 128)]
            + [(1536 + c, False, True, "qT", 512 + c) for c in range(0, 512, 128)]
            + [(512 + c, True, False, "kT", c) for c in range(0, 512, 128)]
            + [(2048 + c, False, False, "kT", 512 + c) for c in range(0, 512, 128)],
            tm=[(1024, 0), (2560, 512)],
            nq=1024,
        )
    return dict(
        C=3656,
        rot=[(0, 2048), (3072, 3584)],
        fm=[(c, True, True, "qT", c) for c in range(0, 1024, 128)]
        + [(1024 + c, True, False, "kT", c) for c in range(0, 1024, 128)]
        + [(3072 + c, True, True, "iqT", c) for c in range(0, 512, 128)],
        tm=[(2048, 0), (2560, 512)],
        nq=1024,
    )


def build_inproj(layer, T, nc=None, io=None):
    cx = Ctx(nc, io)
    nc, S = cx.nc, cx.S
    sp = inproj_spec(layer)
    C = sp["C"]
    nchunk = T // CH
    x = cx.din("x", [T, D])
    w = cx.din("w", [D, C])
    g = cx.din("g", [1, D])
    ident_d = cx.din("ident", [128, 128])
    cq_d = cx.din("cq", [128, T])
    sq_d = cx.din("sq", [128, T])
    ck_d = cx.din("ck", [128, T])
    sk_d = cx.din("sk", [128, T])
    outs = {}
    outs["qT"] = cx.dout("qT", [1024, T], BF16)
    outs["kT"] = cx.dout("kT", [1024, T], BF16)
    outs["v"] = cx.dout("v", [T, 1024], BF16) if layer == 0 else cx.dout("v", [T, 16, 65], BF16)
    if layer == 0:
        bf_d = cx.din("bf", [8, 1])
        outs["nlogf"] = cx.dout("nlogf", [8, T], F32)
    else:
        outs["iqT"] = cx.dout("iqT", [512, T], BF16)
        outs["ikT"] = cx.dout("ikT", [64, T], BF16)
        outs["iw"] = cx.dout("iw", [T, 8], F32)
        lng_d = cx.din("lng", [1, 64])
        lnb_d = cx.din("lnb", [1, 64])
        ctk_d = cx.din("ctk", [T, 8])
        stk_d = cx.din("stk", [T, 8])

    rot_cols = []
    for a, b in sp["rot"]:
        rot_cols.append((a, b, sum(bb - aa for aa, bb in sp["rot"] if bb <= a)))
    nrot = sum(b - a for a, b in sp["rot"])

    def rot_off(col):
        for a, b, off in rot_cols:
            if a <= col < b:
                return off + col - a
        raise KeyError

    with cx.es:
        W = cx.sb([128, 8, C], BF16, "W")
        W2 = cx.sb([128, 8, nrot], BF16, "W2")
        gbc = cx.sb([128, D], F32, "gbc")
        ident = cx.sb([128, 128], BF16, "ident")
        xt = [cx.sb([128, D], F32, f"xt{i}") for i in range(2)]
        junk = cx.sb([128, D], BF16, "junk")
        hn = [cx.sb([128, D], BF16, f"hn{i}") for i in range(2)]
        hnT = [cx.sb([128, 8, CH], BF16, f"hnT{i}") for i in range(2)]
        small = [[cx.sb([128, 1], F32, f"sm{i}_{k}") for k in range(3)] for i in range(2)]
        tabs = [[cx.sb([128, CH], F32, f"tab{i}_{k}") for k in range(4)] for i in range(2)]
        t1 = [cx.sb([128, CH], F32, f"t1_{i}") for i in range(2)]
        t2 = [cx.sb([128, CH], F32, f"t2_{i}") for i in range(2)]
        stg = [cx.sb([128, CH], BF16, f"stg{i}") for i in range(4)]
        tp = [cx.ps([128, D], BF16, f"tp{i}") for i in range(2)]
        fm_ps = [cx.ps([128, CH], F32, f"fm{i}") for i in range(2)]
        fm2_ps = [cx.ps([128, CH], F32, f"fm2{i}") for i in range(2)]
        tm_ps = [cx.ps([128, CH], F32, f"tm{i}") for i in range(2)]

        b_W, b_W2, b_g, b_id = S.buf("W"), S.buf("W2"), S.buf("g"), S.buf("id")
        b_xt, b_junk, b_hn, b_hnT = S.bufs("xt", 2), S.buf("junk"), S.bufs("hn", 2), S.bufs("hnT", 2)
        b_small, b_tabs = S.bufs("small", 2), S.bufs("tabs", 2)
        b_t1, b_t2, b_stg = S.bufs("t1", 2), S.bufs("t2", 2), S.bufs("stg", 4)
        b_tp, b_fm, b_fm2, b_tm = S.bufs("tp", 2), S.bufs("fm", 2), S.bufs("fm2", 2), S.bufs("tm", 2)

        wv = w.rearrange("(k p) c -> p k c", p=128)
        for kc in range(8):
            S.dma("pool", W[:, kc, :], wv[:, kc, :], writes=[b_W], key="W")
        S.dma("pool", ident[:], ident_d, writes=[b_id], key="W")
        S.dma("sp", gbc[:], g.partition_broadcast(128), writes=[b_g], key="g")
        S.op("dve", lambda e: e.memset(W2[:], 0.0), writes=[b_W2])
        for a, b, off in rot_cols:
            nh = (b - a) // HD
            src = W[:, :, a:b].rearrange("p k (h d) -> p k h d", d=HD)
            dst = W2[:, :, off:off + (b - a)].rearrange("p k (h d) -> p k h d", d=HD)
            for kc in range(8):
                S.op("dve", (lambda kc, src, dst: lambda e: e.tensor_scalar(
                    out=dst[:, kc, :, 0:8], in0=src[:, kc, :, 8:16], scalar1=-1.0, scalar2=None, op0=ALU.mult))(kc, src, dst),
                    reads=[b_W], writes=[b_W2])
                S.op("dve", (lambda kc, src, dst: lambda e: e.tensor_copy(
                    out=dst[:, kc, :, 8:16], in_=src[:, kc, :, 0:8]))(kc, src, dst),
                    reads=[b_W], writes=[b_W2])
        if layer == 0:
            negb = cx.sb([8, 1], F32, "negb")
            b_negb = S.buf("negb")
            S.dma("sp", negb[:], bf_d, writes=[b_negb], key="g")
            S.op("dve", lambda e: e.tensor_scalar(out=negb[:], in0=negb[:], scalar1=-1.0, scalar2=None, op0=ALU.mult),
                 reads=[b_negb], writes=[b_negb])
            ef = cx.sb([8, CH], F32, "ef")
            b_ef = S.buf("ef")
            lf = [cx.sb([8, CH], F32, f"lf{i}") for i in range(2)]
            b_lf = S.bufs("lf", 2)
        else:
            lng = cx.sb([128, 64], F32, "lng")
            lnb = cx.sb([128, 64], F32, "lnb")
            b_ln = S.buf("ln")
            S.dma("sp", lng[:], lng_d.partition_broadcast(128), writes=[b_ln], key="g")
            S.dma("sp", lnb[:], lnb_d.partition_broadcast(128), writes=[b_ln], key="g")
            ctk = cx.sb([128, T // 128, 8], F32, "ctk")
            stk = cx.sb([128, T // 128, 8], F32, "stk")
            S.dma("sp", ctk[:], ctk_d.rearrange("(n p) e -> p n e", p=128), writes=[b_ln], key="g")
            S.dma("sp", stk[:], stk_d.rearrange("(n p) e -> p n e", p=128), writes=[b_ln], key="g")
            stgv = [cx.sb([128, 8, 65], BF16, f"stgv{i}") for i in range(2)]
            b_stgv = S.bufs("stgv", 2)
            for i in range(2):
                S.op("pool", (lambda i: lambda e: e.memset(stgv[i][:], 1.0))(i), writes=[b_stgv[i]])
            ik = [cx.sb([128, 72], F32, f"ik{i}") for i in range(2)]
            ikw = [[cx.sb([128, 64], F32, f"ikw{i}_{k}") for k in range(3)] for i in range(2)]
            iks = [[cx.sb([128, 1], F32, f"iks{i}_{k}") for k in range(4)] for i in range(2)]
            ikb = [cx.sb([128, 64], BF16, f"ikb{i}") for i in range(2)]
            ikTs = [cx.sb([64, CH], BF16, f"ikTs{i}") for i in range(2)]
            iwo = [cx.sb([128, 8], F32, f"iwo{i}") for i in range(2)]
            b_ik, b_ikb, b_ikTs, b_iwo = S.bufs("ik", 2), S.bufs("ikb", 2), S.bufs("ikTs", 2), S.bufs("iwo", 2)

        ntile = 0
        nfm = 0
        ntm = 0
        nst = 0
        nsv = 0
        for j in range(nchunk):
            cp = j % 2
            for k, tdram in enumerate((cq_d, sq_d, ck_d, sk_d)):
                S.dma("sp", tabs[cp][k][:], tdram[:, j * CH:(j + 1) * CH], writes=[b_tabs[cp]], key=f"tab{cp}")
            for tt in range(4):
                sl = ntile % 2
                ntile += 1
                r0 = j * CH + tt * 128
                S.dma("sp", xt[sl][:], x[r0:r0 + 128, :], writes=[b_xt[sl]], key=f"x{sl}")
                emit_norm_T(cx, xt[sl], b_xt[sl], gbc, ident, hn[sl], b_hn[sl], tp[sl], b_tp[sl],
                            hnT[cp][:, :, tt * 128:(tt + 1) * 128], b_hnT[cp], small[sl], b_small[sl], junk, b_junk, cb=[b_g, b_id])
                for (c0, oc0) in sp["tm"]:
                    pb = ntm % 2
                    ntm += 1
                    for kc in range(8):
                        S.op("pe", (lambda kc, pb, c0, cp, tt: lambda e: e.matmul(
                            tm_ps[pb][:], lhsT=hnT[cp][:, kc, tt * 128:(tt + 1) * 128], rhs=W[:, kc, c0:c0 + CH],
                            start=(kc == 0), stop=(kc == 7)))(kc, pb, c0, cp, tt),
                            reads=[b_hnT[cp], b_W], writes=[b_tm[pb]])
                    if layer == 0:
                        so = nst % 4
                        nst += 1
                        S.op("act", (lambda pb, so: lambda e: e.copy(out=stg[so][:], in_=tm_ps[pb][:]))(pb, so),
                             reads=[b_tm[pb]], writes=[b_stg[so]])
                        S.dma("sp", outs["v"][r0:r0 + 128, oc0:oc0 + CH], stg[so][:], reads=[b_stg[so]], key=f"st{so}")
                    else:
                        so = nsv % 2
                        nsv += 1
                        S.op("act", (lambda pb, so: lambda e: e.copy(out=stgv[so][:, :, 0:64],
                                                                     in_=tm_ps[pb][:].rearrange("p (h d) -> p h d", d=64)))(pb, so),
                             reads=[b_tm[pb]], writes=[b_stgv[so]])
                        S.dma("sp", outs["v"][r0:r0 + 128, oc0 // 64:oc0 // 64 + 8, :], stgv[so][:], reads=[b_stgv[so]], key=f"sv{so}")
                if layer == 1:
                    q = sl
                    pb = ntm % 2
                    ntm += 1
                    for kc in range(8):
                        S.op("pe", (lambda kc, cp, tt, pb: lambda e: e.matmul(
                            tm_ps[pb][:, 0:72], lhsT=hnT[cp][:, kc, tt * 128:(tt + 1) * 128], rhs=W[:, kc, 3584:3656],
                            start=(kc == 0), stop=(kc == 7)))(kc, cp, tt, pb),
                            reads=[b_hnT[cp], b_W], writes=[b_tm[pb]])
                    S.op("dve", (lambda q, pb: lambda e: e.tensor_copy(out=ik[q][:], in_=tm_ps[pb][:, 0:72]))(q, pb),
                         reads=[b_tm[pb]], writes=[b_ik[q]])
                    S.op("dve", (lambda q: lambda e: e.tensor_scalar(out=iwo[q][:], in0=ik[q][:, 64:72],
                                                                   scalar1=float(8 ** -0.5), scalar2=None, op0=ALU.mult))(q),
                         reads=[b_ik[q]], writes=[b_iwo[q]])
                    S.dma("sp", outs["iw"][r0:r0 + 128, :], iwo[q][:], reads=[b_iwo[q]], key=f"iw{q}")
                    mean, var, rs, nmean = iks[q]
                    xc, sqj, y = ikw[q]
                    S.op("dve", (lambda q, mean: lambda e: e.tensor_scalar(
                        out=ikw[q][1][:], in0=ik[q][:, 0:64], scalar1=1.0 / 64, scalar2=None, op0=ALU.mult,
                        op1=ALU.add, accum_out=mean[:]))(q, mean),
                        reads=[b_ik[q]], writes=[b_ik[q]])
                    S.op("dve", (lambda q, mean, xc: lambda e: e.tensor_scalar(
                        out=xc[:], in0=ik[q][:, 0:64], scalar1=mean[:], scalar2=None, op0=ALU.subtract))(q, mean, xc),
                        reads=[b_ik[q]], writes=[b_ik[q]])
                    S.op("act", (lambda xc, sqj, var: lambda e: e.activation(out=sqj[:], in_=xc[:], func=AF.Square,
                                                                         accum_out=var[:]))(xc, sqj, var),
                         reads=[b_ik[q]], writes=[b_ik[q]])
                    S.op("act", (lambda var, rs: lambda e: e.activation(out=rs[:], in_=var[:], func=AF.Sqrt,
                                                                    scale=1.0 / 64, bias=EPS))(var, rs),
                         reads=[b_ik[q]], writes=[b_ik[q]])
                    S.op("dve", (lambda rs: lambda e: e.reciprocal(out=rs[:], in_=rs[:]))(rs),
                         reads=[b_ik[q]], writes=[b_ik[q]])
                    S.op("dve", (lambda xc, rs, y: lambda e: e.scalar_tensor_tensor(
                        out=y[:], in0=xc[:], scalar=rs[:], in1=lng[:], op0=ALU.mult, op1=ALU.mult))(xc, rs, y),
                        reads=[b_ik[q], b_ln], writes=[b_ik[q]])
                    S.op("dve", (lambda y: lambda e: e.tensor_tensor(out=y[:], in0=y[:], in1=lnb[:], op=ALU.add))(y),
                         reads=[b_ik[q], b_ln], writes=[b_ik[q]])
                    ti = r0 // 128
                    S.op("dve", (lambda y, xc, ti: lambda e: e.tensor_tensor(out=xc[:, 0:8], in0=y[:, 0:8], in1=ctk[:, ti, :], op=ALU.mult))(y, xc, ti),
                         reads=[b_ik[q], b_ln], writes=[b_ik[q]])
                    S.op("dve", (lambda y, xc, ti: lambda e: e.tensor_tensor(out=xc[:, 8:16], in0=y[:, 8:16], in1=stk[:, ti, :], op=ALU.mult))(y, xc, ti),
                         reads=[b_ik[q], b_ln], writes=[b_ik[q]])
                    S.op("dve", (lambda y, xc, ti: lambda e: e.tensor_tensor(out=xc[:, 16:24], in0=y[:, 8:16], in1=ctk[:, ti, :], op=ALU.mult))(y, xc, ti),
                         reads=[b_ik[q], b_ln], writes=[b_ik[q]])
                    S.op("dve", (lambda y, xc, ti: lambda e: e.tensor_tensor(out=xc[:, 24:32], in0=y[:, 0:8], in1=stk[:, ti, :], op=ALU.mult))(y, xc, ti),
                         reads=[b_ik[q], b_ln], writes=[b_ik[q]])
                    S.op("dve", (lambda y, xc: lambda e: e.tensor_tensor(out=y[:, 0:8], in0=xc[:, 0:8], in1=xc[:, 8:16], op=ALU.subtract))(y, xc),
                         reads=[b_ik[q]], writes=[b_ik[q]])
                    S.op("dve", (lambda y, xc: lambda e: e.tensor_tensor(out=y[:, 8:16], in0=xc[:, 16:24], in1=xc[:, 24:32], op=ALU.add))(y, xc),
                         reads=[b_ik[q]], writes=[b_ik[q]])
                    S.op("dve", (lambda y, q: lambda e: e.tensor_copy(out=ikb[q][:], in_=y[:]))(y, q),
                         reads=[b_ik[q]], writes=[b_ikb[q]])
                    S.op("pe", (lambda q, sl: lambda e: e.transpose(out=tp[sl][0:64, 0:128], in_=ikb[q][:], identity=ident[:]))(q, sl),
                         reads=[b_ikb[q], b_id], writes=[b_tp[sl]])
                    S.op("act", (lambda cp, tt, sl: lambda e: e.copy(out=ikTs[cp][:, tt * 128:(tt + 1) * 128], in_=tp[sl][0:64, 0:128]))(cp, tt, sl),
                         reads=[b_tp[sl]], writes=[b_ikTs[cp]])
            if layer == 1:
                S.dma("sp", outs["ikT"][:, j * CH:(j + 1) * CH], ikTs[cp][:], reads=[b_ikTs[cp]], key=f"ikT{cp}")
            for (c0, rot, qs, oname, orow) in sp["fm"]:
                if cx.fused and layer == 1 and j >= nchunk // 2 and oname in ("qT", "iqT"):
                    continue
                pb = nfm % 2
                nfm += 1
                for kc in range(8):
                    S.op("pe", (lambda kc, pb, c0, cp: lambda e: e.matmul(
                        fm_ps[pb][:], lhsT=W[:, kc, c0:c0 + 128], rhs=hnT[cp][:, kc, :],
                        start=(kc == 0), stop=(kc == 7)))(kc, pb, c0, cp),
                        reads=[b_hnT[cp], b_W], writes=[b_fm[pb]])
                so = nst % 4
                nst += 1
                if rot:
                    ro = rot_off(c0)
                    for kc in range(8):
                        S.op("pe", (lambda kc, pb, ro, cp: lambda e: e.matmul(
                            fm2_ps[pb][:], lhsT=W2[:, kc, ro:ro + 128], rhs=hnT[cp][:, kc, :],
                            start=(kc == 0), stop=(kc == 7)))(kc, pb, ro, cp),
                            reads=[b_hnT[cp], b_W2], writes=[b_fm2[pb]])
                    ct, st_ = (tabs[cp][0], tabs[cp][1]) if qs else (tabs[cp][2], tabs[cp][3])
                    S.op("dve", (lambda pb, ct: lambda e: e.tensor_tensor(out=t1[pb][:], in0=fm_ps[pb][:], in1=ct[:], op=ALU.mult))(pb, ct),
                         reads=[b_fm[pb], b_tabs[cp]], writes=[b_t1[pb]])
                    S.op("dve", (lambda pb, st_: lambda e: e.tensor_tensor(out=t2[pb][:], in0=fm2_ps[pb][:], in1=st_[:], op=ALU.mult))(pb, st_),
                         reads=[b_fm2[pb], b_tabs[cp]], writes=[b_t2[pb]])
                    S.op("pool", (lambda pb, so: lambda e: e.tensor_tensor(out=stg[so][:], in0=t1[pb][:], in1=t2[pb][:], op=ALU.add))(pb, so),
                         reads=[b_t1[pb], b_t2[pb]], writes=[b_stg[so]])
                else:
                    sc = 0.125 if qs else 1.0
                    S.op("act", (lambda pb, so, sc: lambda e: e.mul(out=stg[so][:], in_=fm_ps[pb][:], mul=sc))(pb, so, sc),
                         reads=[b_fm[pb]], writes=[b_stg[so]])
                S.dma("sp", outs[oname][orow:orow + 128, j * CH:(j + 1) * CH], stg[so][:], reads=[b_stg[so]], key=f"st{so}")
            if layer == 0:
                pb = nfm % 2
                nfm += 1
                for kc in range(8):
                    S.op("pe", (lambda kc, pb, cp: lambda e: e.matmul(
                        fm_ps[pb][0:8, :], lhsT=W[:, kc, 3072:3080], rhs=hnT[cp][:, kc, :],
                        start=(kc == 0), stop=(kc == 7)))(kc, pb, cp),
                        reads=[b_hnT[cp], b_W], writes=[b_fm[pb]])
                S.op("act", (lambda pb: lambda e: e.activation(out=ef[:], in_=fm_ps[pb][0:8, :], func=AF.Exp, scale=-1.0, bias=negb[:]))(pb),
                     reads=[b_fm[pb], b_negb], writes=[b_ef])
                S.op("act", (lambda cp: lambda e: e.activation(out=lf[cp][:], in_=ef[:], func=AF.Ln, bias=1.0))(cp),
                     reads=[b_ef], writes=[b_lf[cp]])
                S.dma("sp", outs["nlogf"][:, j * CH:(j + 1) * CH], lf[cp][:], reads=[b_lf[cp]], key=f"lf{cp}")
        fin = [k for k in S.dma_cnt if k.startswith(("st", "lf", "iw", "ikT", "sv"))]
        S.emit(final_dma_keys=fin)
    return nc


def zone_masks(r):
    m = np.zeros((8, 128, CH), np.float32)
    k = np.arange(128)[:, None]
    q = np.arange(CH)[None, :]
    for z in range(8):
        m[z] = ((128 * z + k) <= (CH * r + q)).astype(np.float32)
    return m


def zone_negmasks(r):
    return ((zone_masks(r) - 1.0) * 30000.0).astype(np.float32)


class AttnCore:
    def __init__(self, cx, identf, b_identf, tq=None, b_tq=None, tq_stride=129, tq_per_bank=2, npt=4, single_acc=False, n_st=2):
        self.cx = cx
        S = cx.S
        self.st = [cx.ps([128, CH], F32, f"st{i}") for i in range(n_st)]
        self.b_st = S.bufs("st", n_st)
        self.acc = [cx.ps([128, 512], F32, f"acc{i}") for i in range(4)]
        self.b_acc = S.bufs("acc", 4)
        self.sts, self.b_sts = list(self.st), list(self.b_st)
        if single_acc:
            self.sts += [self.acc[1], self.acc[3]]
            self.b_sts += [self.b_acc[1], self.b_acc[3]]
        self.pt = [cx.sb([128, CH], BF16, f"pt{i}") for i in range(npt)]
        self.b_pt = S.bufs("pt", npt)
        self.oT = [cx.sb([128, CH], F32, f"oT{i}") for i in range(4)]
        self.b_oT = S.bufs("oT", 4)
        if tq is None:
            tq = [cx.ps([128, 512], F32, f"tq{i}") for i in range(1)]
            b_tq = S.bufs("tq", 1)
        self.tq, self.b_tq = tq, b_tq
        self.tq_stride, self.tq_per_bank = tq_stride, tq_per_bank
        self.identf, self.b_identf = identf, b_identf
        self.npt = npt
        self.nblk = 0
        self.ngrp = -1
        self.queue = []
        self.look = len(self.sts) - 1

    def set_mode(self, single_acc):
        assert not self.queue
        self.sts, self.b_sts = list(self.st), list(self.b_st)
        if single_acc:
            self.sts += [self.acc[1], self.acc[3]]
            self.b_sts += [self.b_acc[1], self.b_acc[3]]
        self.look = len(self.sts) - 1

    def tq_view(self, qs, ncol):
        b = (qs // self.tq_per_bank) % len(self.tq)
        o = (qs % self.tq_per_bank) * self.tq_stride
        return self.tq[b][:, o:o + ncol], self.b_tq[b]

    def _issue_s(self, blk):
        S = self.cx.S
        i = self.nblk
        self.nblk += 1
        sb = i % len(self.sts)
        blk["sb"] = sb
        blk["pi"] = i % self.npt
        if blk["first"]:
            self.ngrp += 1
        blk["grp"] = self.ngrp
        st = self.sts[sb]
        am = blk.get("addmask")
        S.op("pe", lambda e: e.matmul(st[:], lhsT=blk["k"], rhs=blk["q"], start=True, stop=(am is None)),
             reads=blk["kq_bufs"], writes=[self.b_sts[sb]])
        if am is not None:
            S.op("pe", lambda e: e.matmul(st[:], lhsT=am[0], rhs=am[1], start=False, stop=True),
                 reads=list(am[2]), writes=[self.b_sts[sb]])

    def _finish(self, blk):
        S = self.cx.S
        sb, pi = blk["sb"], blk["pi"]
        st, pt = self.sts[sb], self.pt[pi]
        gb = 2 * (blk["grp"] % 2)
        S.op("act", lambda e: e.activation(out=pt[:], in_=st[:], func=AF.Exp),
             reads=[self.b_sts[sb]], writes=[self.b_pt[pi]])
        for (eng, mk, mb) in blk.get("masks", ()):
            S.op(eng, (lambda mk: lambda e: e.tensor_tensor(out=pt[:], in0=pt[:], in1=mk, op=ALU.mult))(mk),
                 reads=[self.b_pt[pi]] + list(mb), writes=[self.b_pt[pi]])
        for n, (rows, vap, coff) in enumerate(blk["pv"]):
            acc = self.acc[gb + n]
            S.op("pe", (lambda acc, rows, vap: lambda e: e.matmul(acc[0:rows, :], lhsT=vap, rhs=pt[:],
                                                                 start=blk["first"], stop=blk["last"]))(acc, rows, vap),
                 reads=[self.b_pt[pi]] + blk["v_bufs"], writes=[self.b_acc[gb + n]])
        if blk["last"]:
            ncol = sum(r for r, _, _ in blk["pv"])
            for n, (rows, vap, coff) in enumerate(blk["pv"]):
                acc, oT = self.acc[gb + n], self.oT[gb + n]
                S.op("act", (lambda acc, oT, rows: lambda e: e.copy(out=oT[0:rows, :], in_=acc[0:rows, :]))(acc, oT, rows),
                     reads=[self.b_acc[gb + n]], writes=[self.b_oT[gb + n]])
            views = {}
            for pair in ((0, 1), (2, 3)):
                for qs in pair:
                    tv, tb = self.tq_view(qs, ncol)
                    for n, (rows, vap, coff) in enumerate(blk["pv"]):
                        oT = self.oT[gb + n]
                        S.op("pe", (lambda tv, oT, rows, coff, qs: lambda e: e.transpose(
                            out=tv[:, coff:coff + rows], in_=oT[0:rows, qs * 128:(qs + 1) * 128], identity=self.identf[0:rows, 0:rows]))(tv, oT, rows, coff, qs),
                            reads=[self.b_oT[gb + n], self.b_identf], writes=[tb])
                    views[qs] = (tv, tb)
                blk["epilogue"](views, pair)

    def block(self, blk):
        self._issue_s(blk)
        self.queue.append(blk)
        while len(self.queue) > self.look:
            self._finish(self.queue.pop(0))

    def flush(self):
        while self.queue:
            self._finish(self.queue.pop(0))


LAMBDA_INIT0 = 0.8 - 0.6 * float(np.exp(-0.3 * 0))


def ktile_local(j, kt, NT):
    if kt < 4 * j:
        return kt
    if kt < 8 * j:
        return NT // 2 + (kt - 4 * j)
    if kt < 8 * j + 4:
        return 4 * j + (kt - 8 * j)
    return NT // 2 + 4 * j + (kt - 8 * j - 4)


def zone_negmasks_local(r):
    m = np.zeros((2, 8, 128, CH), np.float32)
    k = np.arange(128)[:, None]
    q = np.arange(CH)[None, :]
    for z in range(8):
        tri = np.where((128 * (z % 4) + k) <= q, 0.0, -30000.0)
        if z < 4:
            m[0, z] = tri
            m[1, z] = 0.0 if r == 0 else -30000.0
        else:
            m[0, z] = 0.0 if r == 1 else -30000.0
            m[1, z] = tri
    return m


def build_attn0(SL, nc=None, io=None):
    T = SL
    NT = SL // 128
    NP = SL // 1024
    cx = Ctx(nc, io)
    nc, S = cx.nc, cx.S
    qT = cx.din("qT", [1024, T], BF16)
    kT = cx.din("kT", [1024, SL], BF16)
    v = cx.din("v", [SL, 1024], BF16)
    nlogf = cx.din("nlogf", [8, SL])
    masks_d = cx.din("masks", [2, 8, 128, CH])
    ident_d = cx.din("ident", [128, 128])
    rsel_d = cx.din("rsel", [8, 2])
    lam_d = cx.din("lam", [1, 256])
    subg_d = cx.din("subg", [1, 128])
    mixed = cx.dout("mixed", [T, 1024], BF16)
    cK = cx.scratch("cK", [3, 8, SL], BF16)
    cQ = cx.scratch("cQ", [3, 8, T], BF16)
    PC = 1024

    with cx.es:
        KA = [cx.sb([128, SL], BF16, f"KA{i}") for i in range(2)]
        QA = [cx.sb([128, T], BF16, f"QA{i}") for i in range(2)]
        VA = [cx.sb([128, NT, 129], BF16, f"VA{i}") for i in range(2)]
        VB = [cx.sb([128, NT, 65], BF16, f"VB{i}") for i in range(2)]
        b_KA, b_QA, b_VA, b_VB = S.bufs("KA", 2), S.bufs("QA", 2), S.bufs("VA", 2), S.bufs("VB", 2)
        mk = cx.sb([128, 2, 8, CH], BF16, "mk")
        b_mk = S.buf("mk")
        identf = cx.sb([128, 128], F32, "identf")
        b_identf = S.buf("identf")
        S.dma("sp", identf[:], ident_d, writes=[b_identf], key="idf")
        core = AttnCore(cx, identf, b_identf, n_st=3, npt=6)
        ident = cx.sb([128, 128], BF16, "ident")
        for sd in range(2):
            S.dma("pool", mk[:, sd], masks_d[sd].rearrange("z p q -> p z q"), writes=[b_mk], key="mk")
        S.dma("pool", ident[:], ident_d, writes=[b_mk], key="mk")
        for i in range(2):
            S.op("pool", (lambda i: lambda e: e.memset(VA[i][:, :, 128:129], 1.0))(i), writes=[b_VA[i]])
            S.op("pool", (lambda i: lambda e: e.memset(VB[i][:, :, 64:65], 1.0))(i), writes=[b_VB[i]])
        lam_t = cx.sb([128, 4, 64], F32, "lam_t")
        lam_p = cx.sb([128, 2, 64], F32, "lam_p")
        lam_s = cx.sb([128, 2], F32, "lam_s")
        lam_e = cx.sb([128, 2], F32, "lam_e")
        nlam = cx.sb([128, 1], F32, "nlam")
        gsub = cx.sb([128, 128], F32, "gsub")
        b_lam = S.buf("lam")
        S.dma("sp", lam_t[:].rearrange("p f d -> p (f d)"), lam_d.partition_broadcast(128), writes=[b_lam], key="cst")
        S.dma("sp", gsub[:], subg_d.partition_broadcast(128), writes=[b_lam], key="cst")
        S.op("dve", lambda e: e.tensor_tensor(out=lam_p[:, 0, :], in0=lam_t[:, 0, :], in1=lam_t[:, 1, :], op=ALU.mult),
             reads=[b_lam], writes=[b_lam])
        S.op("dve", lambda e: e.tensor_tensor(out=lam_p[:, 1, :], in0=lam_t[:, 2, :], in1=lam_t[:, 3, :], op=ALU.mult),
             reads=[b_lam], writes=[b_lam])
        S.op("dve", lambda e: e.reduce_sum(out=lam_s[:], in_=lam_p[:], axis=AX.X), reads=[b_lam], writes=[b_lam])
        S.op("act", lambda e: e.activation(out=lam_e[:], in_=lam_s[:], func=AF.Exp), reads=[b_lam], writes=[b_lam])
        S.op("dve", lambda e: e.tensor_tensor(out=nlam[:], in0=lam_e[:, 1:2], in1=lam_e[:, 0:1], op=ALU.subtract),
             reads=[b_lam], writes=[b_lam])
        S.op("dve", lambda e: e.tensor_scalar(out=nlam[:], in0=nlam[:], scalar1=-LAMBDA_INIT0, scalar2=None, op0=ALU.add),
             reads=[b_lam], writes=[b_lam])
        S.op("dve", lambda e: e.tensor_scalar(out=gsub[:], in0=gsub[:], scalar1=1.0 - LAMBDA_INIT0, scalar2=None, op0=ALU.mult),
             reads=[b_lam], writes=[b_lam])

        rsel = cx.sb([8, 2], F32, "rsel")
        ones = cx.sb([8, PC], F32, "ones")
        nl = [cx.sb([8, PC], F32, f"nl{i}") for i in range(2)]
        cp = [cx.sb([8, PC], F32, f"cp{i}") for i in range(2)]
        cqf = cx.sb([8, CH], F32, "cqf")
        r1 = cx.sb([8, PC], F32, "r1")
        tri = [cx.sb([8, 3, PC], BF16, f"tri{i}") for i in range(1)] * 2
        triq = None
        b_c = S.buf("c")
        b_nl, b_cp, b_tri, b_triq = S.bufs("nl", 2), S.bufs("cp", 2), S.bufs("tri", 1) * 2, None
        b_cK, b_cQ = S.buf("cK"), S.buf("cQ")
        S.dma("sp", rsel[:], rsel_d, writes=[b_c], key="cst2")
        S.op("dve", lambda e: e.memset(ones[:], 1.0), writes=[b_c])

        def split3(src, dst, n, bsrc, bdst):
            S.op("dve", lambda e: e.tensor_copy(out=dst[:, 0, :], in_=src), reads=[bsrc], writes=[bdst])
            S.op("dve", lambda e: e.tensor_tensor(out=r1[:, 0:n], in0=src, in1=dst[:, 0, :], op=ALU.subtract),
                 reads=[bsrc, bdst], writes=[b_c])
            S.op("dve", lambda e: e.tensor_copy(out=dst[:, 1, :], in_=r1[:, 0:n]), reads=[b_c], writes=[bdst])
            S.op("dve", lambda e: e.tensor_tensor(out=r1[:, 0:n], in0=r1[:, 0:n], in1=dst[:, 1, :], op=ALU.subtract),
                 reads=[b_c, bdst], writes=[b_c])
            S.op("dve", lambda e: e.tensor_copy(out=dst[:, 2, :], in_=r1[:, 0:n]), reads=[b_c], writes=[bdst])

        gl = cx.sb([8, PC], F32, "gl")
        lc = cx.sb([8, PC], F32, "lc")
        triq2 = [cx.sb([8, 3, PC], BF16, f"triq2_{i}") for i in range(1)] * 2
        b_triq2 = S.bufs("triq2", 1) * 2

        def blend(out, A, B, ca, cb, rd, wr):
            S.op("dve", lambda e: e.tensor_scalar(out=cqf[:], in0=A, scalar1=rsel[:, ca:ca + 1], scalar2=None, op0=ALU.mult),
                 reads=rd + [b_c], writes=[b_c])
            S.op("dve", lambda e: e.scalar_tensor_tensor(out=out, in0=B, scalar=rsel[:, cb:cb + 1], in1=cqf[:], op0=ALU.mult, op1=ALU.add),
                 reads=rd + [b_c], writes=wr)

        for p in range(NP):
            s = p % 2
            S.dma("sp", nl[s][:, 0:CH], nlogf[:, p * CH:(p + 1) * CH], writes=[b_nl[s]], key=f"nl{s}")
            S.dma("sp", nl[s][:, CH:PC], nlogf[:, SL // 2 + p * CH:SL // 2 + (p + 1) * CH], writes=[b_nl[s]], key=f"nl{s}")
            blend(gl[:, 0:CH], nl[s][:, 0:CH], nl[s][:, CH:PC], 0, 1, [b_nl[s]], [b_c])
            blend(gl[:, CH:PC], nl[s][:, 0:CH], nl[s][:, CH:PC], 1, 0, [b_nl[s]], [b_c])
            init = 0.0 if p == 0 else cp[1 - s][:, PC - 1:PC]
            S.op("dve", (lambda s, init: lambda e: e.tensor_tensor_scan(out=cp[s][:], data0=ones[:], data1=gl[:], initial=init,
                                                                     op0=ALU.mult, op1=ALU.add))(s, init),
                 reads=[b_c, b_cp[1 - s]], writes=[b_cp[s]])
            blend(lc[:, 0:CH], cp[s][:, 0:CH], cp[s][:, CH:PC], 0, 1, [b_cp[s]], [b_c])
            blend(lc[:, CH:PC], cp[s][:, 0:CH], cp[s][:, CH:PC], 1, 0, [b_cp[s]], [b_c])
            S.op("dve", (lambda s: lambda e: e.tensor_scalar(out=nl[s][:], in0=lc[:], scalar1=-1.0, scalar2=None, op0=ALU.mult))(s),
                 reads=[b_c], writes=[b_nl[s]])
            split3(nl[s][:], triq2[s], PC, b_nl[s], b_triq2[s])
            S.dma("sp", cQ[:, :, p * CH:(p + 1) * CH].rearrange("t h n -> h t n"), triq2[s][:, :, 0:CH], reads=[b_triq2[s]], writes=[b_cQ], key=f"cq{s}")
            S.dma("sp", cQ[:, :, SL // 2 + p * CH:SL // 2 + (p + 1) * CH].rearrange("t h n -> h t n"), triq2[s][:, :, CH:PC], reads=[b_triq2[s]], writes=[b_cQ], key=f"cq{s}")
            split3(lc[:], tri[s], PC, b_c, b_tri[s])
            S.dma("sp", cK[:, :, p * CH:(p + 1) * CH].rearrange("t h n -> h t n"), tri[s][:, :, 0:CH], reads=[b_tri[s]], writes=[b_cK], key=f"ck{s}")
            S.dma("sp", cK[:, :, SL // 2 + p * CH:SL // 2 + (p + 1) * CH].rearrange("t h n -> h t n"), tri[s][:, :, CH:PC], reads=[b_tri[s]], writes=[b_cK], key=f"ck{s}")

        o1 = [cx.sb([128, 128], F32, f"o1_{i}") for i in range(4)]
        oa = [cx.sb([128, 128], F32, f"oa{i}") for i in range(2)]
        sq = cx.sb([128, 128], F32, "sqj")
        rec = [cx.sb([128, 4], F32, f"rec{i}") for i in range(4)]
        ost = [cx.sb([128, 128], BF16, f"ost{i}") for i in range(4)]
        b_o1, b_oa, b_rec, b_ost, b_sq = S.bufs("o1", 4), S.bufs("oa", 2), S.bufs("rec", 4), S.bufs("ost", 4), S.buf("sq")
        nost = [0]

        units = [("A", h) for h in range(4)] + [("B", h) for h in range(8)]

        def load_unit(ui):
            kind, h = units[ui]
            s = ui % 2
            if kind == "A":
                S.dma("sp", KA[s][:, :], kT[128 * h:128 * h + 128, :], writes=[b_KA[s]], key=f"K{s}")
                S.dma("sp", QA[s][:, :], qT[128 * h:128 * h + 128, :], writes=[b_QA[s]], key=f"Q{s}")
                S.dma("sp", VA[s][:, :, 0:128], v[:, 128 * h:128 * h + 128].rearrange("(n p) d -> p n d", p=128),
                      writes=[b_VA[s]], key=f"V{s}")
            else:
                if h < 2:
                    S.op("pool", (lambda s: lambda e: e.memset(KA[s][64:70, :], 1.0))(s), writes=[b_KA[s]])
                    S.op("pool", (lambda s: lambda e: e.memset(QA[s][64:70, :], 1.0))(s), writes=[b_QA[s]])
                S.dma("sp", KA[s][0:64, :], kT[512 + 64 * h:512 + 64 * h + 64, :], writes=[b_KA[s]], key=f"K{s}")
                S.dma("sp", KA[s][67:70, :], cK[:, h, :], reads=[b_cK], writes=[b_KA[s]], key=f"K{s}")
                S.dma("sp", QA[s][0:64, :], qT[512 + 64 * h:512 + 64 * h + 64, :], writes=[b_QA[s]], key=f"Q{s}")
                S.dma("sp", QA[s][64:67, :], cQ[:, h, :], reads=[b_cQ], writes=[b_QA[s]], key=f"Q{s}")
                S.dma("sp", VB[s][:, :, 0:64], v[:, 512 + 64 * h:512 + 64 * h + 64].rearrange("(n p) d -> p n d", p=128),
                      writes=[b_VB[s]], key=f"VB{s}")

        def epi_A0(j, h):
            def f(views, qss):
                for qs in qss:
                    acc, bacc = views[qs]
                    S.op("dve", (lambda qs, acc: lambda e: e.reciprocal(out=rec[qs][:, 0:1], in_=acc[:, 128:129]))(qs, acc),
                         reads=[bacc], writes=[b_rec[qs]])
                    S.op("dve", (lambda qs, acc: lambda e: e.tensor_scalar(out=o1[qs][:], in0=acc[:, 0:128], scalar1=rec[qs][:, 0:1],
                                                                         scalar2=None, op0=ALU.mult))(qs, acc),
                         reads=[bacc, b_rec[qs]], writes=[b_o1[qs]])
            return f

        def epi_A1(j, h):
            def f(views, qss):
                for qs in qss:
                    acc, bacc = views[qs]
                    a = qs % 2
                    so = nost[0] % 4
                    nost[0] += 1
                    r0 = j * CH + qs * 128
                    S.op("dve", (lambda qs, acc: lambda e: e.reciprocal(out=rec[qs][:, 1:2], in_=acc[:, 128:129]))(qs, acc),
                         reads=[bacc], writes=[b_rec[qs]])
                    S.op("dve", (lambda qs: lambda e: e.tensor_tensor(out=rec[qs][:, 1:2], in0=rec[qs][:, 1:2], in1=nlam[:], op=ALU.mult))(qs),
                         reads=[b_rec[qs], b_lam], writes=[b_rec[qs]])
                    S.op("dve", (lambda qs, acc, a: lambda e: e.scalar_tensor_tensor(out=oa[a][:], in0=acc[:, 0:128], scalar=rec[qs][:, 1:2],
                                                                                   in1=o1[qs][:], op0=ALU.mult, op1=ALU.add))(qs, acc, a),
                         reads=[bacc, b_rec[qs], b_o1[qs]], writes=[b_oa[a]])
                    S.op("act", (lambda qs, a: lambda e: e.activation(out=sq[:], in_=oa[a][:], func=AF.Square, accum_out=rec[qs][:, 2:3]))(qs, a),
                         reads=[b_oa[a]], writes=[b_sq, b_rec[qs]])
                    S.op("act", (lambda qs: lambda e: e.activation(out=rec[qs][:, 3:4], in_=rec[qs][:, 2:3], func=AF.Sqrt, scale=1.0 / 128, bias=EPS))(qs),
                         reads=[b_rec[qs]], writes=[b_rec[qs]])
                    S.op("dve", (lambda qs: lambda e: e.reciprocal(out=rec[qs][:, 3:4], in_=rec[qs][:, 3:4]))(qs),
                         reads=[b_rec[qs]], writes=[b_rec[qs]])
                    S.op("dve", (lambda qs, a, so: lambda e: e.scalar_tensor_tensor(out=ost[so][:], in0=oa[a][:], scalar=rec[qs][:, 3:4], in1=gsub[:],
                                                                                  op0=ALU.mult, op1=ALU.mult))(qs, a, so),
                         reads=[b_oa[a], b_rec[qs], b_lam], writes=[b_ost[so]])
                    S.dma("sp", mixed[r0:r0 + 128, 128 * h:128 * h + 128], ost[so][:], reads=[b_ost[so]], key=f"o{so}")
            return f

        def epi_B(j, h):
            def f(views, qss):
                for qs in qss:
                    acc, bacc = views[qs]
                    so = nost[0] % 4
                    nost[0] += 1
                    r0 = j * CH + qs * 128
                    S.op("dve", (lambda qs, acc: lambda e: e.reciprocal(out=rec[qs][:, 0:1], in_=acc[:, 64:65]))(qs, acc),
                         reads=[bacc], writes=[b_rec[qs]])
                    S.op("dve", (lambda qs, acc, so: lambda e: e.tensor_scalar(out=ost[so][:, 0:64], in0=acc[:, 0:64], scalar1=rec[qs][:, 0:1],
                                                                             scalar2=None, op0=ALU.mult))(qs, acc, so),
                         reads=[bacc, b_rec[qs]], writes=[b_ost[so]])
                    S.dma("sp", mixed[r0:r0 + 128, 512 + 64 * h:512 + 64 * h + 64], ost[so][:, 0:64], reads=[b_ost[so]], key=f"o{so}")
            return f

        Qz = [[cx.sb([128, CH], BF16, f"Qz{a}_{c}") for c in range(2)] for a in range(2)]
        b_Qz = [S.bufs(f"Qz{a}_", 2) for a in range(2)]
        for a in range(2):
            for c in range(2):
                S.op("pool", (lambda a, c: lambda e: e.memset(Qz[a][c][:], 0.0))(a, c), writes=[b_Qz[a][c]])
        nqz = 0
        load_unit(0)
        for ui, (kind, h) in enumerate(units):
            s = ui % 2
            core.flush()
            core.set_mode(kind == "B")
            if ui + 1 < len(units):
                load_unit(ui + 1)
            for side in range(2):
              for j in range(NP):
                lq = side * NP + j
                nk = 8 * (j + 1)
                maps = (0, 1) if kind == "A" else (0,)
                if kind == "A":
                    za = nqz % 2
                    nqz += 1
                    for c in range(2):
                        S.op("pool", (lambda za, c, s, lq: lambda e: e.tensor_copy(
                            out=Qz[za][c][c * 64:(c + 1) * 64, :], in_=QA[s][c * 64:(c + 1) * 64, lq * CH:(lq + 1) * CH]))(za, c, s, lq),
                            reads=[b_QA[s]], writes=[b_Qz[za][c]])
                for c in maps:
                    for kt in range(nk):
                        lt = ktile_local(j, kt, NT)
                        if kind == "A":
                            kap = KA[s][:, lt * 128:(lt + 1) * 128]
                            qap = Qz[za][c][:, :]
                            qzb = b_Qz[za][c]
                            vb = b_VA[s]
                            pv = [(65, VA[s][:, lt, 64:129], 64), (64, VA[s][:, lt, 0:64], 0)]
                            epi = epi_A0(lq, h) if c == 0 else epi_A1(lq, h)
                        else:
                            kap = KA[s][0:70, lt * 128:(lt + 1) * 128]
                            qap = QA[s][0:70, lq * CH:(lq + 1) * CH]
                            qzb = b_QA[s]
                            vb = b_VB[s]
                            pv = [(65, VB[s][:, lt, 0:65], 0)]
                            epi = epi_B(lq, h)
                        blk = dict(k=kap, q=qap, kq_bufs=[b_KA[s], qzb], pv=pv, v_bufs=[vb],
                                   first=(kt == 0), last=(kt == nk - 1), epilogue=epi)
                        if kt >= nk - 8:
                            blk["addmask"] = (ident[:], mk[:, side, kt - (nk - 8), :], [b_mk])
                        core.block(blk)
        core.flush()
        S.emit(final_dma_keys=[f"o{i}" for i in range(4)])
    return nc


def build_mlp(T, final, nc=None, io=None):
    MC = 256
    cx = Ctx(nc, io)
    nc, S = cx.nc, cx.S
    mixed = cx.din("mixed", [T, 1024], BF16)
    xres = cx.din("xres", [T, D])
    wo_d = cx.din("wo", [D, D])
    w1_d = cx.din("w1", [D, 4 * D])
    w2_d = cx.din("w2", [4 * D, D])
    g_d = cx.din("g", [1, D])
    ident_d = cx.din("ident", [128, 128])
    if final:
        gf_d = cx.din("gf", [1, D])
    hout = cx.dout("hout", [T, D])
    with cx.es:
        WO = cx.sb([128, 8, D], BF16, "WO")
        W1 = cx.sb([128, 8, 4 * D], BF16, "W1")
        W2 = cx.sb([128, 32, D], BF16, "W2")
        aT = cx.sb([128, 32, MC], BF16, "aT")
        hb = [cx.sb([128, D], F32, f"hb{i}") for i in range(4)]
        mx = [cx.sb([128, D], BF16, f"mx{i}") for i in range(2)]
        hnT = [cx.sb([128, 8, MC], BF16, f"hnT{i}") for i in range(2)]
        hn = [cx.sb([128, D], BF16, f"hn{i}") for i in range(2)]
        rl = [cx.sb([128, MC], F32, f"rl{i}") for i in range(2)]
        gbc = cx.sb([128, D], F32, "gbc")
        ident = cx.sb([128, 128], BF16, "ident")
        small = [[cx.sb([128, 1], F32, f"sm{i}_{k}") for k in range(3)] for i in range(2)]
        tp = [cx.ps([128, D], BF16, f"tp{i}") for i in range(2)]
        acc = [cx.ps([128, 512], F32, f"acc{i}") for i in range(2)]
        up = [cx.ps([128, 512], F32, f"up{i}") for i in range(3)]
        b_WO, b_W1, b_W2, b_aT, b_g, b_id = S.buf("WO"), S.buf("W1"), S.buf("W2"), S.buf("aT"), S.buf("g"), S.buf("id")
        b_hb, b_mx, b_hnT, b_hn, b_rl = S.bufs("hb", 4), S.bufs("mx", 2), S.bufs("hnT", 2), S.bufs("hn", 2), S.bufs("rl", 2)
        b_small, b_tp, b_acc, b_up = S.bufs("small", 2), S.bufs("tp", 2), S.bufs("acc", 2), S.bufs("up", 3)
        if final:
            gfbc = cx.sb([128, D], F32, "gfbc")
            S.dma("sp", gfbc[:], gf_d.partition_broadcast(128), writes=[b_g], key="g")
        S.dma("sp", gbc[:], g_d.partition_broadcast(128), writes=[b_g], key="g")
        S.dma("pool", ident[:], ident_d, writes=[b_id], key="id")
        wov = wo_d.rearrange("(k p) c -> p k c", p=128)
        for kc in range(8):
            S.dma("pool", WO[:, kc, :], wov[:, kc, :], writes=[b_WO], key="WO")
        w1v = w1_d.rearrange("(k p) c -> p k c", p=128)
        for kc in range(8):
            S.dma("pool", W1[:, kc, :], w1v[:, kc, :], writes=[b_W1], key="W1")
        w2v = w2_d.rearrange("(k p) c -> p k c", p=128)
        for fc in range(0, 32, 4):
            S.dma("pool", W2[:, fc:fc + 4, :], w2v[:, fc:fc + 4, :], writes=[b_W2], key="W2")

        nacc = 0
        nup = 0
        for ci in range(T // MC):
            cp = ci % 2
            for tt in range(2):
                r0 = ci * MC + tt * 128
                hi = cp * 2 + tt
                m = tt
                S.dma("sp", mx[m][:], mixed[r0:r0 + 128, :], writes=[b_mx[m]], key=f"mx{m}")
                S.dma("sp", hb[hi][:], xres[r0:r0 + 128, :], writes=[b_hb[hi]], key=f"hb{hi}")
                for kc in range(8):
                    S.op("pe", (lambda kc, m: lambda e: e.transpose(out=tp[m][:, kc * 128:(kc + 1) * 128],
                                                                   in_=mx[m][:, kc * 128:(kc + 1) * 128], identity=ident[:]))(kc, m),
                         reads=[b_mx[m], b_id], writes=[b_tp[m]])
                S.op("act", (lambda m, cp, tt: lambda e: e.copy(out=hnT[cp][:, :, tt * 128:(tt + 1) * 128],
                                                              in_=tp[m][:].rearrange("p (k t) -> p k t", k=8)))(m, cp, tt),
                     reads=[b_tp[m]], writes=[b_hnT[cp]])
                for half in range(2):
                    a = nacc % 2
                    nacc += 1
                    for kc in range(8):
                        S.op("pe", (lambda kc, a, cp, tt, half: lambda e: e.matmul(
                            acc[a][:], lhsT=hnT[cp][:, kc, tt * 128:(tt + 1) * 128], rhs=WO[:, kc, half * 512:(half + 1) * 512],
                            start=(kc == 0), stop=(kc == 7)))(kc, a, cp, tt, half),
                            reads=[b_hnT[cp], b_WO], writes=[b_acc[a]])
                    S.op("dve", (lambda a, hi, half: lambda e: e.tensor_tensor(
                        out=hb[hi][:, half * 512:(half + 1) * 512], in0=acc[a][:], in1=hb[hi][:, half * 512:(half + 1) * 512], op=ALU.add))(a, hi, half),
                        reads=[b_acc[a], b_hb[hi]], writes=[b_hb[hi]])
            for tt in range(2):
                hi = cp * 2 + tt
                m = tt
                emit_norm_T(cx, hb[hi], b_hb[hi], gbc, ident, hn[m], b_hn[m], tp[m], b_tp[m],
                            hnT[cp][:, :, tt * 128:(tt + 1) * 128], b_hnT[cp], small[m], b_small[m], mx[m], b_mx[m], cb=[b_g, b_id])
            for fb in range(32):
                u = nup % 3
                nup += 1
                r = fb % 2
                for kc in range(8):
                    S.op("pe", (lambda kc, u, fb, cp: lambda e: e.matmul(
                        up[u][:, 0:MC], lhsT=W1[:, kc, fb * 128:(fb + 1) * 128], rhs=hnT[cp][:, kc, :],
                        start=(kc == 0), stop=(kc == 7)))(kc, u, fb, cp),
                        reads=[b_hnT[cp], b_W1], writes=[b_up[u]])
                S.op("act", (lambda u, r: lambda e: e.activation(out=rl[r][:], in_=up[u][:, 0:MC], func=AF.Relu))(u, r),
                     reads=[b_up[u]], writes=[b_rl[r]])
                S.op("pool", (lambda r, fb: lambda e: e.tensor_tensor(out=aT[:, fb, :], in0=rl[r][:], in1=rl[r][:], op=ALU.mult))(r, fb),
                     reads=[b_rl[r]], writes=[b_aT])
            for tt in range(2):
                hi = cp * 2 + tt
                r0 = ci * MC + tt * 128
                for half in range(2):
                    a = nacc % 2
                    nacc += 1
                    for fc in range(32):
                        S.op("pe", (lambda fc, a, tt, half: lambda e: e.matmul(
                            acc[a][:], lhsT=aT[:, fc, tt * 128:(tt + 1) * 128], rhs=W2[:, fc, half * 512:(half + 1) * 512],
                            start=(fc == 0), stop=(fc == 31)))(fc, a, tt, half),
                            reads=[b_aT, b_W2], writes=[b_acc[a]])
                    S.op("dve", (lambda a, hi, half: lambda e: e.tensor_tensor(
                        out=hb[hi][:, half * 512:(half + 1) * 512], in0=acc[a][:], in1=hb[hi][:, half * 512:(half + 1) * 512], op=ALU.add))(a, hi, half),
                        reads=[b_acc[a], b_hb[hi]], writes=[b_hb[hi]])
                if final:
                    m = tt
                    ss, sd, rstd = small[m]
                    S.op("act", (lambda hi, m, ss: lambda e: e.activation(out=mx[m][:], in_=hb[hi][:], func=AF.Square, accum_out=ss[:]))(hi, m, ss),
                         reads=[b_hb[hi]], writes=[b_mx[m], b_small[m]])
                    S.op("act", (lambda ss, sd: lambda e: e.activation(out=sd[:], in_=ss[:], func=AF.Sqrt, scale=1.0 / D, bias=EPS))(ss, sd),
                         reads=[b_small[m]], writes=[b_small[m]])
                    S.op("dve", (lambda sd, rstd: lambda e: e.reciprocal(out=rstd[:], in_=sd[:]))(sd, rstd),
                         reads=[b_small[m]], writes=[b_small[m]])
                    S.op("dve", (lambda hi, rstd: lambda e: e.scalar_tensor_tensor(out=hb[hi][:], in0=hb[hi][:], scalar=rstd[:], in1=gfbc[:],
                                                                                 op0=ALU.mult, op1=ALU.mult))(hi, rstd),
                         reads=[b_hb[hi], b_small[m], b_g], writes=[b_hb[hi]])
                S.dma("sp", hout[r0:r0 + 128, :], hb[hi][:], reads=[b_hb[hi]], key=f"ho{hi}")
        S.emit(final_dma_keys=[f"ho{i}" for i in range(4)])
    return nc


U8 = mybir.dt.uint8
NBIS = 20
BIS0 = 16.0
TOPK = 256


def zone_negmasks_q(r):
    m = np.zeros((128, 4, 1024), np.float32)
    q = np.arange(128)[:, None]
    zk = np.arange(CH)[None, :]
    for qs in range(4):
        m[:, qs, 0:CH] = np.where(zk <= 128 * qs + q, 0.0, -30000.0)
        m[:, qs, CH:] = 0.0 if r == 1 else -30000.0
    return m


def build_dsa(SL, nc=None, io=None):
    T = SL // 2
    nchunk = T // CH
    NP = SL // 1024
    NT = SL // 128
    NKMAX = SL
    cx = Ctx(nc, io)
    nc, S = cx.nc, cx.S
    qT = cx.din("qT", [1024, SL], BF16)
    kT = cx.din("kT", [1024, SL], BF16)
    v = cx.din("v", [SL, 16, 65], BF16)
    iqT = cx.din("iqT", [512, SL], BF16)
    ikT = cx.din("ikT", [64, SL], BF16)
    iw_d = cx.din("iw", [SL, 8])
    nmq_d = cx.din("nmq", [128, 4, 1024])
    ident_d = cx.din("ident", [128, 128])
    mixed = cx.dout("mixed", [T, 1024], BF16)
    with cx.es:
        K2 = [cx.sb([128, NKMAX], BF16, f"K2{i}") for i in range(2)]
        Q2 = [[cx.sb([128, CH], BF16, f"Q2{i}_{hh}") for hh in range(2)] for i in range(2)]
        V2 = [cx.sb([128, NKMAX // 128, 130], BF16, f"V2{i}") for i in range(2)]
        IQ = [cx.sb([128, 4, CH], BF16, f"IQ{i}") for i in range(2)]
        IK = cx.sb([128, SL], BF16, "IK")
        IW = cx.sb([128, T // 128, 8], F32, "IW")
        sc = cx.sb([128, NKMAX], F32, "sc")
        msk = cx.sb([128, NKMAX], BF16, "msk")
        MT = cx.sb([128, NKMAX // 128, CH], U8, "MT")
        nmq = cx.sb([128, 4, 1024], BF16, "nmq")
        ident = cx.sb([128, 128], BF16, "ident")
        rl = [cx.sb([128, CH], F32, f"rl{i}") for i in range(2)]
        bs = [cx.sb([128, 1], F32, f"bs{i}") for i in range(4)]
        rec = [cx.sb([128, 1], F32, f"rec{i}") for i in range(4)]
        ost = [cx.sb([128, 64], BF16, f"ost{i}") for i in range(4)]
        lg0 = cx.ps([128, CH], F32, "lg0")
        tpm = cx.ps([128, 1024], BF16, "tpm")
        b_lg0, b_tpm = S.buf("lg0"), S.buf("tpm")
        identf = cx.sb([128, 128], F32, "identf")
        b_identf = S.buf("identf")
        S.dma("sp", identf[:], ident_d, writes=[b_identf], key="idf")
        core = AttnCore(cx, identf, b_identf, tq=[lg0], b_tq=[b_lg0], tq_stride=65, tq_per_bank=4, npt=6, single_acc=True)
        lgs = [lg0, core.st[0], core.st[1]]
        b_lgs = [b_lg0, core.b_st[0], core.b_st[1]]
        b_K2, b_Q2, b_V2, b_IQ = S.bufs("K2", 2), S.bufs("Q2", 2), S.bufs("V2", 2), S.bufs("IQ", 2)
        b_IK, b_IW, b_sc, b_msk, b_MT, b_cst = S.buf("IK"), S.buf("IW"), S.buf("sc"), S.buf("msk"), S.buf("MT"), S.buf("cst")
        b_rl, b_bs, b_rec, b_ost = S.bufs("rl", 2), S.buf("bs"), S.bufs("rec", 4), S.bufs("ost", 4)

        for i in range(2):
            for hh in range(2):
                S.op("pool", (lambda i, hh: lambda e: e.memset(Q2[i][hh][:], 0.0))(i, hh), writes=[b_Q2[i]])
        S.dma("pool", nmq[:], nmq_d, writes=[b_cst], key="cst")
        S.dma("pool", ident[:], ident_d, writes=[b_cst], key="cst")
        S.dma("sp", IK[0:64, :], ikT, writes=[b_IK], key="IK")
        S.dma("sp", IK[64:128, :], ikT, writes=[b_IK], key="IK")
        S.dma("sp", IW[:], iw_d[0:T, :].rearrange("(n p) e -> p n e", p=128), writes=[b_IW], key="IW")
        lo, mid, cnt, flag = bs
        nlg = 0
        nrl = 0
        nost = 0
        nmm = 0
        nunit = 0

        def load_unit(j, hp, s):
            n1 = CH * (j + 1)
            for off in (0, SL // 2):
                S.dma("sp", K2[s][:, off:off + n1], kT[hp * 128:(hp + 1) * 128, off:off + n1], writes=[b_K2[s]], key=f"K{s}")
                S.dma("sp", V2[s][:, off // 128:(off + n1) // 128, :],
                      v[off:off + n1, 2 * hp:2 * hp + 2, :].rearrange("(n p) h d -> p n (h d)", p=128),
                      writes=[b_V2[s]], key=f"V{s}")
            for hh in range(2):
                S.dma("sp", Q2[s][hh][hh * 64:(hh + 1) * 64, :], qT[hp * 128 + hh * 64:hp * 128 + (hh + 1) * 64, j * CH:(j + 1) * CH],
                      writes=[b_Q2[s]], key=f"Q{s}")

        for j in range(nchunk):
            NK = 1024 * (j + 1)
            nkt = NK // 128
            iqs = j % 2
            S.dma("sp", IQ[iqs][:], iqT[:, j * CH:(j + 1) * CH].rearrange("(h p) t -> p h t", p=128), writes=[b_IQ[iqs]], key=f"IQ{iqs}")
            load_unit(j, 0, nunit % 2)
            for qs in range(4):
                ti = 4 * j + qs
                for kc in range(NK // CH):
                    lch = kc if kc < j else (NP + kc - j if kc < 2 * j else (j if kc == 2 * j else NP + j))
                    for h in range(8):
                        hp, hh = h // 2, h % 2
                        li = nlg % 3
                        nlg += 1
                        ri = nrl % 2
                        nrl += 1
                        S.op("pe", (lambda li, hp, hh, lch, qs, iqs: lambda e: e.matmul(
                            lgs[li][:], lhsT=IQ[iqs][hh * 64:(hh + 1) * 64, hp, qs * 128:(qs + 1) * 128],
                            rhs=IK[hh * 64:(hh + 1) * 64, lch * CH:(lch + 1) * CH], start=True, stop=True))(li, hp, hh, lch, qs, iqs),
                            reads=[b_IQ[iqs], b_IK], writes=[b_lgs[li]])
                        S.op("act", (lambda li, ri: lambda e: e.activation(out=rl[ri][:], in_=lgs[li][:], func=AF.Relu))(li, ri),
                             reads=[b_lgs[li]], writes=[b_rl[ri]])
                        if h == 0:
                            S.op("dve", (lambda ri, kc, ti: lambda e: e.tensor_scalar(
                                out=sc[:, kc * CH:(kc + 1) * CH], in0=rl[ri][:], scalar1=IW[:, ti, 0:1], scalar2=None,
                                op0=ALU.mult))(ri, kc, ti),
                                reads=[b_rl[ri], b_IW], writes=[b_sc])
                        else:
                            S.op("dve", (lambda ri, kc, ti, h: lambda e: e.scalar_tensor_tensor(
                                out=sc[:, kc * CH:(kc + 1) * CH], in0=rl[ri][:], scalar=IW[:, ti, h:h + 1], in1=sc[:, kc * CH:(kc + 1) * CH],
                                op0=ALU.mult, op1=ALU.add))(ri, kc, ti, h),
                                reads=[b_rl[ri], b_IW, b_sc], writes=[b_sc])
                S.op("dve", (lambda NK, qs: lambda e: e.tensor_tensor(out=sc[:, NK - 1024:NK], in0=sc[:, NK - 1024:NK], in1=nmq[:, qs, :], op=ALU.add))(NK, qs),
                     reads=[b_sc, b_cst], writes=[b_sc])
                S.op("dve", lambda e: e.memset(lo[:], -BIS0), writes=[b_bs])
                for it in range(NBIS):
                    step = BIS0 / (2 ** it)
                    S.op("dve", (lambda step: lambda e: e.tensor_scalar(out=mid[:], in0=lo[:], scalar1=step, scalar2=None, op0=ALU.add))(step),
                         reads=[b_bs], writes=[b_bs])
                    S.op("dve", (lambda NK: lambda e: e.tensor_scalar(out=msk[:, 0:NK], in0=sc[:, 0:NK], scalar1=mid[:], scalar2=None,
                                                                     op0=ALU.is_ge, op1=ALU.add, accum_out=cnt[:]))(NK),
                         reads=[b_sc, b_bs], writes=[b_msk, b_bs])
                    S.op("dve", (lambda step: lambda e: e.tensor_scalar(out=flag[:], in0=cnt[:], scalar1=TOPK - 0.5, scalar2=step,
                                                                       op0=ALU.is_ge, op1=ALU.mult))(step),
                         reads=[b_bs], writes=[b_bs])
                    S.op("dve", lambda e: e.tensor_tensor(out=lo[:], in0=lo[:], in1=flag[:], op=ALU.add), reads=[b_bs], writes=[b_bs])
                S.op("dve", (lambda NK: lambda e: e.tensor_scalar(out=msk[:, 0:NK], in0=sc[:, 0:NK], scalar1=lo[:], scalar2=None, op0=ALU.is_ge))(NK),
                     reads=[b_sc, b_bs], writes=[b_msk])
                for k0 in range(0, nkt, 8):
                    for kk in range(8):
                        S.op("pe", (lambda k0, kk: lambda e: e.transpose(out=tpm[:, kk * 128:(kk + 1) * 128],
                                                                        in_=msk[:, (k0 + kk) * 128:(k0 + kk + 1) * 128], identity=ident[:]))(k0, kk),
                             reads=[b_msk, b_cst], writes=[b_tpm])
                    S.op("act", (lambda k0, qs: lambda e: e.copy(out=MT[:, k0:k0 + 8, qs * 128:(qs + 1) * 128],
                                                                in_=tpm[:].rearrange("p (k t) -> p k t", k=8)))(k0, qs),
                         reads=[b_tpm], writes=[b_MT])
            for hp in range(8):
                s = nunit % 2
                nunit += 1
                core.flush()
                if hp + 1 < 8:
                    load_unit(j, hp + 1, nunit % 2)
                for hh in range(2):
                    head = 2 * hp + hh

                    def epi(views, qss, j=j, head=head):
                        nonlocal nost
                        for qs in qss:
                            acc, bacc = views[qs]
                            so = nost % 4
                            nost += 1
                            r0 = j * CH + qs * 128
                            S.op("dve", (lambda qs, acc: lambda e: e.reciprocal(out=rec[qs][:], in_=acc[:, 64:65]))(qs, acc),
                                 reads=[bacc], writes=[b_rec[qs]])
                            S.op("dve", (lambda qs, acc, so: lambda e: e.tensor_scalar(out=ost[so][:], in0=acc[:, 0:64], scalar1=rec[qs][:],
                                                                                     scalar2=None, op0=ALU.mult))(qs, acc, so),
                                 reads=[bacc, b_rec[qs]], writes=[b_ost[so]])
                            S.dma("sp", mixed[r0:r0 + 128, head * 64:(head + 1) * 64], ost[so][:], reads=[b_ost[so]], key=f"o{so}")

                    for kt in range(nkt):
                        meng = "pool" if (nmm % 4 == 0) else "dve"
                        nmm += 1
                        lt = ktile_local(j, kt, NT)
                        blk = dict(k=K2[s][:, lt * 128:(lt + 1) * 128], q=Q2[s][hh][:, :],
                                   kq_bufs=[b_K2[s], b_Q2[s]], pv=[(65, V2[s][:, lt, hh * 65:(hh + 1) * 65], 0)], v_bufs=[b_V2[s]],
                                   first=(kt == 0), last=(kt == nkt - 1), epilogue=epi,
                                   masks=[(meng, MT[:, kt, :], [b_MT])])
                        core.block(blk)
            core.flush()
        S.emit(final_dma_keys=[f"o{i}" for i in range(4)])
    return nc


def build_fused(SL):
    T = SL // 2
    nc = bass.Bass("TRN2", target_bir_lowering=False)

    def ein(name, shape, dt=F32):
        return nc.dram_tensor(name, list(shape), dt, kind="ExternalInput").ap()

    def scr(name, shape, dt):
        return nc.dram_tensor(name, list(shape), dt, kind="Internal").ap()

    E = dict(
        x=ein("x", [SL, D]), w_in0=ein("w_in0", [D, 3080]), w_in1=ein("w_in1", [D, 3656]),
        wo0=ein("wo0", [D, D]), wo1=ein("wo1", [D, D]),
        w1_0=ein("w1_0", [D, 4 * D]), w1_1=ein("w1_1", [D, 4 * D]), w2_0=ein("w2_0", [4 * D, D]), w2_1=ein("w2_1", [4 * D, D]),
        g_mix0=ein("g_mix0", [1, D]), g_mix1=ein("g_mix1", [1, D]), g_mlp0=ein("g_mlp0", [1, D]), g_mlp1=ein("g_mlp1", [1, D]),
        g_final=ein("g_final", [1, D]), bf=ein("bf", [8, 1]), lam=ein("lam", [1, 256]), subg=ein("subg", [1, 128]),
        lng=ein("lng", [1, 64]), lnb=ein("lnb", [1, 64]), ident=ein("ident", [128, 128]),
        cq=ein("cq", [128, SL]), sq=ein("sq", [128, SL]), ck=ein("ck", [128, SL]), sk=ein("sk", [128, SL]),
        ctk=ein("ctk", [SL, 8]), stk=ein("stk", [SL, 8]),
        masks0=ein("masks0", [2, 8, 128, CH]), nmq=ein("nmq", [128, 4, 1024]), rsel=ein("rsel", [8, 2]),
    )
    out = nc.dram_tensor("out", [T, D], F32, kind="ExternalOutput").ap()
    qT0, kT0 = scr("qT0", [1024, SL], BF16), scr("kT0", [1024, SL], BF16)
    v0, nlogf = scr("v0", [SL, 1024], BF16), scr("nlogf", [8, SL], F32)
    mixed0, h1 = scr("mixed0", [SL, 1024], BF16), scr("h1", [SL, D], F32)
    qT1, kT1 = scr("qT1", [1024, SL], BF16), scr("kT1", [1024, SL], BF16)
    v1 = scr("v1", [SL, 16, 65], BF16)
    iqT, ikT, iw = scr("iqT", [512, SL], BF16), scr("ikT", [64, SL], BF16), scr("iw", [SL, 8], F32)
    mixed1 = scr("mixed1", [T, 1024], BF16)
    rope = dict(cq=E["cq"], sq=E["sq"], ck=E["ck"], sk=E["sk"], ident=E["ident"])
    build_inproj(0, SL, nc, dict(rope, x=E["x"], w=E["w_in0"], g=E["g_mix0"], bf=E["bf"], qT=qT0, kT=kT0, v=v0, nlogf=nlogf))
    build_attn0(SL, nc, dict(qT=qT0, kT=kT0, v=v0, nlogf=nlogf, masks=E["masks0"], ident=E["ident"], rsel=E["rsel"],
                             lam=E["lam"], subg=E["subg"], mixed=mixed0))
    build_mlp(SL, False, nc, dict(mixed=mixed0, xres=E["x"], wo=E["wo0"], w1=E["w1_0"], w2=E["w2_0"], g=E["g_mlp0"],
                                  ident=E["ident"], hout=h1))
    build_inproj(1, SL, nc, dict(rope, x=h1, w=E["w_in1"], g=E["g_mix1"], lng=E["lng"], lnb=E["lnb"], ctk=E["ctk"], stk=E["stk"],
                                 qT=qT1, kT=kT1, v=v1, iqT=iqT, ikT=ikT, iw=iw))
    build_dsa(SL, nc, dict(qT=qT1, kT=kT1, v=v1, iqT=iqT, ikT=ikT, iw=iw, nmq=E["nmq"], ident=E["ident"], mixed=mixed1))
    build_mlp(T, True, nc, dict(mixed=mixed1, xres=h1[0:T, :], wo=E["wo1"], w1=E["w1_1"], w2=E["w2_1"], g=E["g_mlp1"],
                                gf=E["g_final"], ident=E["ident"], hout=out))
    return nc


def local_positions(SL, r):
    return np.concatenate([own_positions(SL, r), own_positions(SL, 1 - r)])


_PROG = {}


def make_in_maps(x, norm_mix, w_in_even, b_forget, lambda_q1, lambda_k1, lambda_q2, lambda_k2,
                 diff_subln_g, w_out_even, w_in_odd, idx_ln_g, idx_ln_b, w_out_odd,
                 norm_mlp, w_mlp_in, w_mlp_out, norm_final, ncores=NCORE):
    f32 = np.float32
    x = np.asarray(x, f32)
    B, SL, _ = x.shape
    A = lambda a: np.ascontiguousarray(np.asarray(a, f32))
    shared = dict(
        w_in0=A(w_in_even[0]), w_in1=A(w_in_odd[0]), wo0=A(w_out_even[0]), wo1=A(w_out_odd[0]),
        w1_0=A(w_mlp_in[0]), w1_1=A(w_mlp_in[1]), w2_0=A(w_mlp_out[0]), w2_1=A(w_mlp_out[1]),
        g_mix0=A(norm_mix[0:1]), g_mix1=A(norm_mix[1:2]), g_mlp0=A(norm_mlp[0:1]), g_mlp1=A(norm_mlp[1:2]),
        g_final=A(norm_final).reshape(1, D), bf=A(b_forget[0]).reshape(8, 1),
        lam=np.concatenate([A(a[0]) for a in (lambda_q1, lambda_k1, lambda_q2, lambda_k2)]).reshape(1, 256),
        subg=A(diff_subln_g[0:1]), lng=A(idx_ln_g[0:1]), lnb=A(idx_ln_b[0:1]), ident=np.eye(128, dtype=f32),
    )
    per_r = []
    for r in range(2):
        pos = local_positions(SL, r)
        cq, sq = rope_tables_fm(pos, 0.125)
        ck, sk = rope_tables_fm(pos, 1.0)
        per_r.append(dict(cq=cq, sq=sq, ck=ck, sk=sk, ctk=np.ascontiguousarray(ck[0:8].T), stk=np.ascontiguousarray(sk[0:8].T),
                          masks0=zone_negmasks_local(r), nmq=zone_negmasks_q(r),
                          rsel=np.tile(np.array([[1.0 - r, r]], f32), (8, 1)), pos=pos))
    maps = []
    for c in range(ncores):
        b, r = c // 2, c % 2
        m = dict(shared)
        m.update({k: v for k, v in per_r[r].items() if k != "pos"})
        m["x"] = np.ascontiguousarray(x[b][per_r[r]["pos"]])
        maps.append(m)
    return maps


def kernel(**inputs):
    x = np.asarray(inputs["x"], np.float32)
    B, SL, _ = x.shape
    T = SL // 2
    if ("F", SL) not in _PROG:
        _PROG[("F", SL)] = build_fused(SL)
    nc = _PROG[("F", SL)]
    maps = make_in_maps(**inputs)
    res = run_bass_kernel_spmd(nc, maps, core_ids=list(range(NCORE))).results
    out = np.zeros((B, SL, D), np.float32)
    for c in range(NCORE):
        out[c // 2][own_positions(SL, c % 2)] = np.asarray(res[c]["out"])
    return out
```
